# Optimizing a Trainium2 kernel written in Bass

```python
import jax, jax.numpy as jnp
from jax import lax
import numpy as np

D_MODEL = 1024
BATCH = 4
SEQ = 8192
DEPTH = 4

N_A = DEPTH // 2
N_B = DEPTH - N_A
N_HEADS = 16
HEAD_DIM = D_MODEL // N_HEADS
CONV_W = 3
D_FF = 2816
Q_BLOCK = 128
EPS = 1e-6

kernel_name = "yoco_shortconv_fox_hybrid"


def rmsnorm(x, g):
    xf = x.astype(jnp.float32)
    y = xf * lax.rsqrt(jnp.mean(xf * xf, axis=-1, keepdims=True) + EPS)
    return (y * g).astype(x.dtype)


def causal_dwconv(u, w):
    width = w.shape[0]
    s = u.shape[1]
    up = jnp.pad(u, ((0, 0), (width - 1, 0), (0, 0)))
    return sum(up[:, i:i + s] * w[i] for i in range(width))


def short_conv_mixer(xn, w_in, conv_w, w_out):
    proj = xn @ w_in
    b, c, h = jnp.split(proj, 3, axis=-1)
    u = causal_dwconv(c * h, conv_w)
    return (b * u) @ w_out


def conv_ffn(xn, w_up, conv_w, w_down):
    up = xn @ w_up
    a, g = jnp.split(up, 2, axis=-1)
    a = causal_dwconv(a, conv_w)
    return (jax.nn.silu(a) * g) @ w_down


def forgetting_attention(q, k, v, c):
    bsz, nh, s_len, hd = q.shape
    nb = s_len // Q_BLOCK
    scale = hd ** -0.5
    qb = q.reshape(bsz, nh, nb, Q_BLOCK, hd).transpose(2, 0, 1, 3, 4)
    cb = c.reshape(bsz, nh, nb, Q_BLOCK).transpose(2, 0, 1, 3)
    kpos = jnp.arange(s_len)

    def one_block(args):
        q_i, c_i, i = args
        s = jnp.einsum('bhqd,bhkd->bhqk', q_i, k, preferred_element_type=jnp.float32) * scale
        s = s + c_i[..., None] - c[:, :, None, :]
        qpos = i * Q_BLOCK + jnp.arange(Q_BLOCK)
        s = jnp.where(kpos[None, :] <= qpos[:, None], s, -jnp.inf)
        p = jax.nn.softmax(s, axis=-1)
        return jnp.einsum('bhqk,bhkd->bhqd', p.astype(v.dtype), v)

    o = lax.map(one_block, (qb, cb, jnp.arange(nb)))
    return o.transpose(1, 0, 3, 2, 4).reshape(bsz, s_len, nh * hd)


def setup_inputs(seed: int = 0) -> dict:
    key = jax.random.key(seed)
    ks = jax.random.split(key, 17)
    f32 = jnp.float32
    out_scale = (2 * DEPTH) ** -0.5

    def nrm(k, shape, scale):
        return jax.random.normal(k, shape, f32) * scale

    def gain(k, shape):
        return 1.0 + 0.02 * jax.random.normal(k, shape, f32)

    x = nrm(ks[0], (BATCH, SEQ, D_MODEL), 1.0)
    attn_norm = gain(ks[1], (DEPTH, D_MODEL))
    ffn_norm = gain(ks[2], (DEPTH, D_MODEL))
    a_w_in = nrm(ks[3], (N_A, D_MODEL, 3 * D_MODEL), D_MODEL ** -0.5)
    a_conv = nrm(ks[4], (N_A, CONV_W, D_MODEL), CONV_W ** -0.5)
    a_w_out = nrm(ks[5], (N_A, D_MODEL, D_MODEL), out_scale * D_MODEL ** -0.5)
    kv_norm = gain(ks[6], (D_MODEL,))
    w_kvf = jnp.concatenate([
        nrm(ks[7], (D_MODEL, 2 * D_MODEL), D_MODEL ** -0.5),
        nrm(ks[8], (D_MODEL, N_HEADS), 0.1 * D_MODEL ** -0.5),
    ], axis=1)
    b_f = jax.random.uniform(ks[9], (N_HEADS,), f32, 1.0, 6.0)
    k_norm = gain(ks[10], (HEAD_DIM,))
    b_w_qg = nrm(ks[11], (N_B, D_MODEL, 2 * D_MODEL), D_MODEL ** -0.5)
    q_norm = gain(ks[12], (N_B, HEAD_DIM))
    b_w_out = nrm(ks[13], (N_B, D_MODEL, D_MODEL), out_scale * D_MODEL ** -0.5)
    ffn_w_up = nrm(ks[14], (DEPTH, D_MODEL, 2 * D_FF), D_MODEL ** -0.5)
    ffn_conv = nrm(ks[15], (DEPTH, CONV_W, D_FF), CONV_W ** -0.5)
    ffn_w_down = nrm(ks[16], (DEPTH, D_FF, D_MODEL), out_scale * D_FF ** -0.5)
    return {"x": x, "attn_norm": attn_norm, "ffn_norm": ffn_norm,
            "a_w_in": a_w_in, "a_conv": a_conv, "a_w_out": a_w_out,
            "kv_norm": kv_norm, "w_kvf": w_kvf, "b_f": b_f, "k_norm": k_norm,
            "b_w_qg": b_w_qg, "q_norm": q_norm, "b_w_out": b_w_out,
            "ffn_w_up": ffn_w_up, "ffn_conv": ffn_conv, "ffn_w_down": ffn_w_down}


def reference(x, attn_norm, ffn_norm, a_w_in, a_conv, a_w_out, kv_norm, w_kvf, b_f,
              k_norm, b_w_qg, q_norm, b_w_out, ffn_w_up, ffn_conv, ffn_w_down):
    bsz, s_len, d = x.shape
    k = v = c = None
    for l in range(DEPTH):
        if l < N_A:
            xn = rmsnorm(x, attn_norm[l])
            x = x + short_conv_mixer(xn, a_w_in[l], a_conv[l], a_w_out[l])
        else:
            if l == N_A:
                h = rmsnorm(x, kv_norm)
                kvf = h @ w_kvf
                k_s = kvf[..., :d].reshape(bsz, s_len, N_HEADS, HEAD_DIM)
                v_s = kvf[..., d:2 * d].reshape(bsz, s_len, N_HEADS, HEAD_DIM)
                f_logit = (kvf[..., 2 * d:] + b_f).astype(jnp.float32)
                k = rmsnorm(k_s, k_norm).transpose(0, 2, 1, 3)
                v = v_s.transpose(0, 2, 1, 3)
                c = jnp.cumsum(jax.nn.log_sigmoid(f_logit), axis=1).transpose(0, 2, 1)
            j = l - N_A
            xn = rmsnorm(x, attn_norm[l])
            qg = xn @ b_w_qg[j]
            q = rmsnorm(qg[..., :d].reshape(bsz, s_len, N_HEADS, HEAD_DIM), q_norm[j])
            q = q.transpose(0, 2, 1, 3)
            o = forgetting_attention(q, k, v, c)
            o = o * jax.nn.sigmoid(qg[..., d:])
            x = x + o @ b_w_out[j]
        xn = rmsnorm(x, ffn_norm[l])
        x = x + conv_ffn(xn, ffn_w_up[l], ffn_conv[l], ffn_w_down[l])
    return x
```

```python
import types
import numpy as np
from contextlib import ExitStack
import concourse.bass as bass
import concourse.mybir as mybir
from concourse.bass_utils import run_bass_kernel_spmd

F32 = mybir.dt.float32
BF16 = mybir.dt.bfloat16
AF = mybir.ActivationFunctionType
ALU = mybir.AluOpType

D = 1024
KC = 8
DFF = 2816
FC = 22
NH = 16
HD = 64
HALO = 32
EPS = 1e-6
PEN = -30000.0

C_GATT = 0
C_GFFN = 32
C_GKV = 64
C_ACONV = 72
C_FCONV = 120
C_GK = 384
C_GQ = 385
C_HS = 387
C_BF = 388
C_PEN = 404


def _freeze(fn):
    if fn.__closure__ is None:
        return fn
    cells = []
    for c in fn.__closure__:
        try:
            cells.append(types.CellType(c.cell_contents))
        except ValueError:
            cells.append(c)
    return types.FunctionType(fn.__code__, fn.__globals__, fn.__name__, fn.__defaults__, tuple(cells))


class Op:
    __slots__ = ("eng", "fn", "deps", "dma", "sem_key", "count", "signal", "idx", "inc", "self_inc")


class Sched:
    ENGS = ("pe", "act", "dve", "pool", "sp")

    def __init__(self):
        self.ops = []
        self.last_writer = {}
        self.readers = {}
        self.barrier_op = None
        self.last_on_eng = {}
        self.last_dma_on_sem = {}
        self.groups = []

    def _add(self, eng, fn, r, w, dma, sem_key):
        op = Op()
        op.eng, op.fn, op.dma, op.sem_key = eng, _freeze(fn), dma, sem_key
        op.idx = len(self.ops)
        op.signal = dma
        op.count = 0
        raw = set()
        other = set()
        if self.barrier_op is not None:
            raw.add(self.barrier_op)
        for k in r:
            lw = self.last_writer.get(k)
            if lw is not None:
                raw.add(lw)
        for k in w:
            lw = self.last_writer.get(k)
            if lw is not None:
                other.add(lw)
            for rd in self.readers.get(k, ()):
                other.add(rd)
        for k in r:
            self.readers.setdefault(k, []).append(op.idx)
        for k in w:
            self.last_writer[k] = op.idx
            self.readers[k] = []
        deps = []
        for d in raw | other:
            p = self.ops[d]
            if (not p.dma) and (not dma) and p.eng == eng:
                if eng == "pe":
                    continue
                if d not in raw:
                    continue
            deps.append(d)
        op.deps = deps
        self.ops.append(op)
        if dma:
            self.last_dma_on_sem[sem_key] = op.idx
        else:
            self.last_on_eng[eng] = op.idx
        return op.idx

    def op(self, eng, fn, r=(), w=()):
        return self._add(eng, fn, r, w, False, None)

    def dma(self, eng, fn, sem_key, r=(), w=(), inc=16, self_inc=False):
        i = self._add(eng, fn, r, w, True, sem_key)
        self.ops[i].inc = inc
        self.ops[i].self_inc = self_inc
        return i

    def barrier(self, fn):
        deps = set(self.last_on_eng.values()) | set(self.last_dma_on_sem.values())
        op = Op()
        op.eng, op.fn, op.dma, op.sem_key = "dve", _freeze(fn), False, None
        op.idx = len(self.ops)
        op.signal = False
        op.count = 0
        op.deps = sorted(deps)
        self.ops.append(op)
        self.last_on_eng["dve"] = op.idx
        self.barrier_op = op.idx
        self.last_writer = {}
        self.readers = {}
        return op.idx

    def emit(self, nc, es):
        ops = self.ops
        for op in ops:
            for d in op.deps:
                ops[d].signal = True
        eng_sem = {e: es.enter_context(nc.semaphore("s_" + e)) for e in ("pe", "act", "dve", "pool")}
        dma_sems = {}
        cnt = {e: 0 for e in eng_sem}
        dcnt = {}
        for op in ops:
            if op.dma:
                if op.sem_key not in dma_sems:
                    dma_sems[op.sem_key] = es.enter_context(nc.semaphore("d_%d" % len(dma_sems)))
                    dcnt[op.sem_key] = 0
                dcnt[op.sem_key] += op.inc
                op.count = dcnt[op.sem_key]
            elif op.signal:
                cnt[op.eng] += 1
                op.count = cnt[op.eng]
        for grp in self.groups:
            c = max(ops[i].count for i in grp)
            for i in grp:
                ops[i].count = c
        per_eng = {e: [] for e in self.ENGS}
        for op in ops:
            per_eng[op.eng].append(op)
        final = []
        for k, s in dma_sems.items():
            final.append((s, dcnt[k]))
        for e, s in eng_sem.items():
            if cnt[e]:
                final.append((s, cnt[e]))
        block = es.enter_context(nc.Block())

        def stream(e, handle):
            seen = {}
            for op in per_eng[e]:
                waits = {}
                for d in op.deps:
                    p = ops[d]
                    s = dma_sems[p.sem_key] if p.dma else eng_sem[p.eng]
                    key = id(s)
                    if seen.get(key, 0) >= p.count:
                        continue
                    if key not in waits or waits[key][1] < p.count:
                        waits[key] = (s, p.count)
                for key, (s, c) in waits.items():
                    handle.wait_ge(s, c)
                    seen[key] = c
                if op.dma and op.self_inc:
                    op.fn(handle, dma_sems[op.sem_key])
                    continue
                ins = op.fn(handle)
                if op.dma:
                    ins.then_inc(dma_sems[op.sem_key], op.inc)
                elif op.signal:
                    ins.then_inc(eng_sem[op.eng], 1)
            if e == "sp":
                for s, c in final:
                    handle.wait_ge(s, c)

        @block.tensor
        def _(h):
            stream("pe", h)

        @block.scalar
        def _(h):
            stream("act", h)

        @block.vector
        def _(h):
            stream("dve", h)

        @block.gpsimd
        def _(h):
            stream("pool", h)

        @block.sync
        def _(h):
            stream("sp", h)


class Ring:
    def __init__(self, name, n):
        self.name, self.n, self.i = name, n, -1

    def next(self):
        self.i = (self.i + 1) % self.n
        return self.i


def build_nc(SV, SO, TC, debug=False, n_cores=8):
    assert SV % 512 == 0 and SO % 512 == 0 and TC % 512 == 0 and SO % TC == 0 and SV % TC == 0
    NJ = SV // 128
    QN = SO + HALO
    Q0 = SV - QN
    TB = TC + HALO
    NJo = SO // 128
    assert SV == 2 * SO
    NCONST = C_PEN + NJo
    ND = (NJo + 1) * NH
    pairs = [[2 * i, 2 * i + 1] for i in range(n_cores // 2)]
    RK = min(D, (2 << 20) // (SO * 2))
    RV = min(SO, 1024)
    nc = bass.Bass("TRN2", target_bir_lowering=False)
    S = Sched()

    def din(name, shape, dt=F32):
        return nc.dram_tensor(name, list(shape), dt, kind="ExternalInput").ap()

    def dscr(name, shape, dt):
        if debug:
            return nc.dram_tensor(name, list(shape), dt, kind="ExternalOutput").ap()
        return nc.dram_tensor(name, list(shape), dt).ap()

    xT_d = din("xT", [D, QN])
    consts_d = din("consts", [128, NCONST])
    cmat_d = din("cmat", [128, 384 + 896])
    a_w_in_d = din("a_w_in", [2, D, 3 * D])
    a_w_out_d = din("a_w_out", [2, D, D])
    w_kvf_d = din("w_kvf", [D, 2 * D + NH])
    b_w_qg_d = din("b_w_qg", [2, D, 2 * D])
    b_w_out_d = din("b_w_out", [2, D, D])
    ffn_w_up_d = din("ffn_w_up", [4, D, 2 * DFF])
    ffn_w_down_d = din("ffn_w_down", [4, DFF, D])
    out_d = nc.dram_tensor("outT", [D, SO], F32, kind="ExternalOutput").ap()

    def dcc(name, shape, dt):
        return nc.dram_tensor(name, list(shape), dt).ap()

    Ks_d = dcc("pubK", [D, SO], BF16)
    Vs_d = dcc("pubV", [SO, D], BF16)
    Dd_d = dcc("pubD", [128, ND], F32)
    gK_d = dcc("gathK", [2 * D, SO], BF16)
    gV_d = dcc("gathV", [2 * SO, D], BF16)
    gD_d = dcc("gathD", [256, ND], F32)
    oK_d = dcc("othK", [D, SO], BF16)
    oV_d = dcc("othV", [SO, D], BF16)
    oD_d = dcc("othD", [128, ND], F32)
    CQ_d = dscr("CQ", [NH, QN], BF16)
    Xs_d = dscr("Xs", [D, QN], F32)
    Qs_d = dscr("Qs", [D, QN], BF16)
    Gs_d = dscr("Gs", [D, QN], BF16)
    Os_d = dscr("Os", [D, QN], BF16)

    with ExitStack() as es:
        def sb(name, shape, dt):
            return es.enter_context(nc.sbuf_tensor("sb_" + name, list(shape), dt))

        AX = sb("AX", [128, max(KC * TB, 8192)], F32)
        AA = sb("AA", [128, max(FC * TB, 2 * SV + 4096)], BF16)
        AN = sb("AN", [128, max(KC * TB, 2 * QN)], BF16)
        AW = sb("AW", [128, max(4 * 3072, 2 * QN)], BF16)
        WV = sb("WV", [128, KC, D + NH], BF16)
        consts = sb("consts", [128, NCONST], F32)
        cmat = sb("cmat", [128, 384 + 896], F32)
        ones_bf = sb("ones_bf", [128, 128], BF16)
        bd_bf = sb("bd_bf", [128, 128], BF16)
        mask_bf = sb("mask_bf", [128, 896], BF16)
        bias_all = sb("bias_all", [128, NJ, NH], F32)
        sacc = sb("sacc", [128, NH], F32)
        carryA = sb("carryA", [128, 2, KC, 2], F32)
        carryF = sb("carryF", [128, 4, FC, 2], F32)
        sq_t = sb("sq_t", [128, 3, 512], BF16)
        rs_t = sb("rs_t", [128, 2, 512], F32)
        csb_t = sb("csb_t", [128, 2, 512], F32)
        cv_t = sb("cv_t", [128, 3, 2 + TB], F32)
        u_t = sb("u_t", [128, 2, 512], F32)
        s_t = sb("s_t", [128, 2, 512], F32)
        st_t = sb("st_t", [128, 4, 512], BF16)
        vst_t = sb("vst_t", [128, 2, D], BF16)
        zf_t = sb("zf_t", [128, 2, 3, NH], F32)
        cq_t = sb("cq_t", [NH, 2, TB], BF16)
        dpeer = sb("dpeer", [128, ND], F32)
        totb = sb("totb", [128, NH], F32)
        on_t = sb("on_t", [64, 2, 512], F32)
        rd_t = sb("rd_t", [64, 2, 512], F32)
        scratch = sb("scratch", [128, 8], F32)
        banks = [es.enter_context(nc.psum_tensor("ps%d" % i, [128, 512], F32)) for i in range(8)]

        xT = AX[:, 0:KC * TB].rearrange("p (c t) -> p c t", c=KC)
        xn = AN[:, 0:KC * TB].rearrange("p (c t) -> p c t", c=KC)
        act = AA[:, 0:FC * TB].rearrange("p (c t) -> p c t", c=FC)
        ones_f = cmat[:, 0:128]
        tri_f = cmat[:, 256:384]

        def ccol(i):
            return consts[:, i:i + 1]

        ps_ring = Ring("ps", 8)
        sq_ring = Ring("sq", 3)
        rs_ring = Ring("rs", 2)
        csb_ring = Ring("csb", 2)
        cv_ring = Ring("cv", 3)
        u_ring = Ring("u", 2)
        s_ring = Ring("s", 2)
        st_ring = Ring("st", 4)
        vst_ring = Ring("vst", 2)
        zf_ring = Ring("zf", 2)
        w_ring = Ring("w", 4)

        def bank():
            i = ps_ring.next()
            return banks[i], ("ps", i)

        S.dma("sp", lambda e: e.dma_start(out=consts[:], in_=consts_d), "c0", w=["consts"])
        S.dma("sp", lambda e: e.dma_start(out=cmat[:], in_=cmat_d), "c1", w=["cmat"])
        S.op("dve", lambda e: e.tensor_copy(out=ones_bf[:], in_=cmat[:, 0:128]), r=["cmat"], w=["ones_bf"])
        S.op("dve", lambda e: e.tensor_copy(out=bd_bf[:], in_=cmat[:, 128:256]), r=["cmat"], w=["bd_bf"])
        S.op("dve", lambda e: e.tensor_copy(out=mask_bf[:], in_=cmat[:, 384:384 + 896]), r=["cmat"], w=["mask_bf"])
        S.op("dve", lambda e: e.memset(sacc[:], 0.0), w=["sacc"])
        S.op("dve", lambda e: e.memset(carryA[:], 0.0), w=["carryA"])
        S.op("dve", lambda e: e.memset(carryF[:], 0.0), w=["carryF"])
        S.dma("pool", lambda e: e.dma_start(
            out=WV[:], in_=w_kvf_d[:, D:2 * D + NH].rearrange("(k p) n -> p k n", p=128)), "wv", w=["WV"])
        S.barrier(lambda e: e.memset(scratch[:], 0.0))

        def load_w(view_shape, src_ap):
            slot = w_ring.next()
            n = 1
            for d_ in view_shape[1:]:
                n *= d_
            flat = AW[:, slot * 3072: slot * 3072 + n]
            if len(view_shape) == 3:
                view = flat.rearrange("p (k n) -> p k n", k=view_shape[1])
            else:
                view = flat.rearrange("p (k g n) -> p k g n", k=view_shape[1], g=view_shape[2])
            keys = [("w", slot, 0), ("w", slot, 1), ("w", slot, 2)]
            if len(view_shape) == 3:
                S.dma("pool", lambda e: e.dma_start(out=view, in_=src_ap), ("w", slot), w=keys)
            else:
                ng = view_shape[2]
                grp = []
                for g in range(ng):
                    wk_ = [keys[g]] + ([keys[2]] if (ng == 2 and g == 1) else [])
                    grp.append(S.dma("pool", lambda e: e.dma_start(out=view[:, :, g, :], in_=src_ap[:, :, g, :]),
                                     ("w", slot), w=wk_))
                S.groups.append(grp)
            return view, keys

        def slab_cols(w2d, ngroups, gsize, c0, width=128):
            v = w2d.rearrange("(k p) (g c) -> p k g c", p=128, g=ngroups)
            return v[:, :, :, c0:c0 + width]

        def rmsnorm(subs, gbase):
            for (off, w) in subs:
                pb, pk = bank()
                for c in range(KC):
                    si = sq_ring.next()
                    sk = ("sq", si)
                    S.op("act", lambda e, c=c, si=si: e.activation(
                        out=sq_t[:, si, 0:w], in_=xT[:, c, off:off + w], func=AF.Square),
                        r=[("x", c, off)], w=[sk])
                    S.op("pe", lambda e, c=c, si=si: e.matmul(
                        pb[:, 0:w], ones_bf[:], sq_t[:, si, 0:w], start=(c == 0), stop=(c == KC - 1)),
                        r=[sk, "ones_bf"], w=[pk])
                ri = rs_ring.next()
                rk = ("rs", ri)
                S.op("act", lambda e, ri=ri: e.activation(
                    out=rs_t[:, ri, 0:w], in_=pb[:, 0:w], func=AF.Sqrt, bias=EPS, scale=1.0 / D),
                    r=[pk], w=[rk])
                S.op("dve", lambda e, ri=ri: e.reciprocal(out=rs_t[:, ri, 0:w], in_=rs_t[:, ri, 0:w]),
                     r=[rk], w=[rk])
                for c in range(KC):
                    S.op("dve", lambda e, c=c, ri=ri: e.scalar_tensor_tensor(
                        out=xn[:, c, off:off + w], in0=xT[:, c, off:off + w], scalar=ccol(gbase + c),
                        in1=rs_t[:, ri, 0:w], op0=ALU.mult, op1=ALU.mult),
                        r=[("x", c, off), rk, "consts"], w=[("xn", c, off)])

        def head_norm_rs(pb, pk, w, bias_eps, scale):
            si = sq_ring.next()
            sk = ("sq", si)
            S.op("act", lambda e: e.activation(out=sq_t[:, si, 0:w], in_=pb[:, 0:w], func=AF.Square),
                 r=[pk], w=[sk])
            sb_, sbk = bank()
            S.op("pe", lambda e: e.matmul(sb_[:, 0:w], bd_bf[:], sq_t[:, si, 0:w], start=True, stop=True),
                 r=[sk, "bd_bf"], w=[sbk])
            ri = rs_ring.next()
            rk = ("rs", ri)
            S.op("act", lambda e: e.activation(out=rs_t[:, ri, 0:w], in_=sb_[:, 0:w], func=AF.Sqrt,
                                               bias=bias_eps, scale=scale), r=[sbk], w=[rk])
            S.op("dve", lambda e: e.reciprocal(out=rs_t[:, ri, 0:w], in_=rs_t[:, ri, 0:w]), r=[rk], w=[rk])
            return ri, rk

        def conv_head_init(ci, carry_ap, ckey):
            S.op("act", lambda e: e.activation(out=cv_t[:, ci, 0:2], in_=carry_ap, func=AF.Copy),
                 r=[ckey], w=[("cv", ci, "h")])

        def conv_tail_save(ci, carry_ap, ckey, L, lastoff):
            S.op("act", lambda e: e.activation(out=carry_ap, in_=cv_t[:, ci, L:L + 2], func=AF.Copy),
                 r=[("cv", ci, lastoff)], w=[ckey])

        def mixer(l, subs, L):
            rmsnorm(subs, C_GATT + l * 8)
            for j in range(KC):
                wv, wk = load_w([128, KC, 3, 128], slab_cols(a_w_in_d[l], 3, D, j * 128))
                ci = cv_ring.next()
                ck = ("cA", l, j)
                conv_head_init(ci, carryA[:, l, j, :], ck)
                prev = "h"
                for (off, w) in subs:
                    pbs = [bank() for _ in range(3)]
                    for g in range(3):
                        pb, pk = pbs[g]
                        for k in range(KC):
                            S.op("pe", lambda e, pb=pb, g=g, k=k: e.matmul(
                                pb[:, 0:w], wv[:, k, g, :], xn[:, k, off:off + w], start=(k == 0), stop=(k == KC - 1)),
                                r=wk + [("xn", k, off)], w=[pk])
                    (Bp, Bk), (Cp, Ck), (Hp, Hk) = pbs
                    qi = csb_ring.next()
                    qk = ("csb", qi)
                    S.op("act", lambda e, qi=qi, Cp=Cp: e.activation(out=csb_t[:, qi, 0:w], in_=Cp[:, 0:w], func=AF.Copy),
                         r=[Ck], w=[qk])
                    S.op("dve", lambda e, qi=qi, Hp=Hp: e.tensor_tensor(
                        out=cv_t[:, ci, 2 + off:2 + off + w], in0=Hp[:, 0:w], in1=csb_t[:, qi, 0:w], op=ALU.mult),
                        r=[Hk, qk], w=[("cv", ci, off)])
                    ui = u_ring.next()
                    uk = ("u", ui)
                    base = C_ACONV + l * 24
                    S.op("dve", lambda e, ui=ui: e.tensor_scalar(
                        out=u_t[:, ui, 0:w], in0=cv_t[:, ci, off:off + w], scalar1=ccol(base + j), scalar2=None,
                        op0=ALU.mult), r=[("cv", ci, off), ("cv", ci, prev), "consts"], w=[uk])
                    S.op("dve", lambda e, ui=ui: e.scalar_tensor_tensor(
                        out=u_t[:, ui, 0:w], in0=cv_t[:, ci, off + 1:off + 1 + w], scalar=ccol(base + 8 + j),
                        in1=u_t[:, ui, 0:w], op0=ALU.mult, op1=ALU.add),
                        r=[("cv", ci, off), ("cv", ci, prev), uk], w=[uk])
                    S.op("dve", lambda e, ui=ui: e.scalar_tensor_tensor(
                        out=u_t[:, ui, 0:w], in0=cv_t[:, ci, off + 2:off + 2 + w], scalar=ccol(base + 16 + j),
                        in1=u_t[:, ui, 0:w], op0=ALU.mult, op1=ALU.add),
                        r=[("cv", ci, off), uk], w=[uk])
                    S.op("dve", lambda e, ui=ui, Bp=Bp: e.tensor_tensor(
                        out=act[:, j, off:off + w], in0=Bp[:, 0:w], in1=u_t[:, ui, 0:w], op=ALU.mult),
                        r=[Bk, uk], w=[("act", j, off)])
                    prev = off
                conv_tail_save(ci, carryA[:, l, j, :], ck, L, prev)
            out_proj(a_w_out_d[l], subs)

        def out_proj(w2d, subs):
            for m in range(KC):
                wv, wk = load_w([128, KC, 128], w2d.rearrange("(k p) n -> p k n", p=128)[:, :, m * 128:(m + 1) * 128])
                for (off, w) in subs:
                    pb, pk = bank()
                    for j in range(KC):
                        S.op("pe", lambda e, pb=pb, j=j: e.matmul(
                            pb[:, 0:w], wv[:, j, :], act[:, j, off:off + w], start=(j == 0), stop=(j == KC - 1)),
                            r=wk + [("act", j, off)], w=[pk])
                    S.op("dve", lambda e, pb=pb: e.tensor_tensor(
                        out=xT[:, m, off:off + w], in0=pb[:, 0:w], in1=xT[:, m, off:off + w], op=ALU.add),
                        r=[pk, ("x", m, off)], w=[("x", m, off)])

        def ffn(l, subs, L):
            rmsnorm(subs, C_GFFN + l * 8)
            base = C_FCONV + l * 66
            for fc in range(FC):
                wv, wk = load_w([128, KC, 2, 128], slab_cols(ffn_w_up_d[l], 2, DFF, fc * 128))
                ci = cv_ring.next()
                ck = ("cF", l, fc)
                conv_head_init(ci, carryF[:, l, fc, :], ck)
                prev = "h"
                for (off, w) in subs:
                    (Ap, Ak), (Gp, Gk) = bank(), bank()
                    for g, (pb, pk) in enumerate(((Ap, Ak), (Gp, Gk))):
                        for k in range(KC):
                            S.op("pe", lambda e, pb=pb, g=g, k=k: e.matmul(
                                pb[:, 0:w], wv[:, k, g, :], xn[:, k, off:off + w], start=(k == 0), stop=(k == KC - 1)),
                                r=wk + [("xn", k, off)], w=[pk])
                    S.op("act", lambda e, Ap=Ap: e.activation(
                        out=cv_t[:, ci, 2 + off:2 + off + w], in_=Ap[:, 0:w], func=AF.Copy),
                        r=[Ak], w=[("cv", ci, off)])
                    ui = u_ring.next()
                    uk = ("u", ui)
                    S.op("dve", lambda e, ui=ui: e.tensor_scalar(
                        out=u_t[:, ui, 0:w], in0=cv_t[:, ci, off:off + w], scalar1=ccol(base + fc), scalar2=None,
                        op0=ALU.mult), r=[("cv", ci, off), ("cv", ci, prev), "consts"], w=[uk])
                    S.op("dve", lambda e, ui=ui: e.scalar_tensor_tensor(
                        out=u_t[:, ui, 0:w], in0=cv_t[:, ci, off + 1:off + 1 + w], scalar=ccol(base + 22 + fc),
                        in1=u_t[:, ui, 0:w], op0=ALU.mult, op1=ALU.add),
                        r=[("cv", ci, off), ("cv", ci, prev), uk], w=[uk])
                    S.op("dve", lambda e, ui=ui, Ap=Ap: e.scalar_tensor_tensor(
                        out=u_t[:, ui, 0:w], in0=Ap[:, 0:w], scalar=ccol(base + 44 + fc),
                        in1=u_t[:, ui, 0:w], op0=ALU.mult, op1=ALU.add),
                        r=[Ak, uk], w=[uk])
                    si = s_ring.next()
                    sk = ("s", si)
                    S.op("act", lambda e, ui=ui, si=si: e.activation(
                        out=s_t[:, si, 0:w], in_=u_t[:, ui, 0:w], func=AF.Silu), r=[uk], w=[sk])
                    S.op("dve", lambda e, si=si, Gp=Gp: e.tensor_tensor(
                        out=act[:, fc, off:off + w], in0=Gp[:, 0:w], in1=s_t[:, si, 0:w], op=ALU.mult),
                        r=[Gk, sk], w=[("act", fc, off)])
                    prev = off
                conv_tail_save(ci, carryF[:, l, fc, :], ck, L, prev)
            for m in range(KC):
                wv, wk = load_w([128, FC, 128],
                                ffn_w_down_d[l].rearrange("(k p) n -> p k n", p=128)[:, :, m * 128:(m + 1) * 128])
                for (off, w) in subs:
                    pb, pk = bank()
                    for fc in range(FC):
                        S.op("pe", lambda e, pb=pb, fc=fc: e.matmul(
                            pb[:, 0:w], wv[:, fc, :], act[:, fc, off:off + w], start=(fc == 0), stop=(fc == FC - 1)),
                            r=wk + [("act", fc, off)], w=[pk])
                    S.op("dve", lambda e, pb=pb: e.tensor_tensor(
                        out=xT[:, m, off:off + w], in0=pb[:, 0:w], in1=xT[:, m, off:off + w], op=ALU.add),
                        r=[pk, ("x", m, off)], w=[("x", m, off)])

        ALLOFF = sorted({0, HALO} | {o for o in range(0, TC, 512)} | {HALO + o for o in range(0, TC, 512)})

        def xkeys(c, subs):
            return [("x", c, off) for off in ALLOFF]

        def load_x2(src2d, t0, L, subs):
            for c in range(KC):
                S.dma("sp", lambda e, c=c: e.dma_start(
                    out=xT[:, c, 0:L], in_=src2d[c * 128:(c + 1) * 128, t0:t0 + L]), ("xl", c),
                    w=xkeys(c, subs))

        def store_x(dst2d, t0, L, subs, src_off=0):
            for c in range(KC):
                S.dma("sp", lambda e, c=c: e.dma_start(
                    out=dst2d[c * 128:(c + 1) * 128, t0:t0 + L], in_=xT[:, c, src_off:src_off + L]), ("xs", c),
                    r=xkeys(c, subs))

        def kv_proj(tq0, subs, L, ci_chunk, after_norm=None):
            rmsnorm(subs, C_GKV)
            if after_norm is not None:
                after_norm()
            ksubs = [(off, w) for (off, w) in subs if w == 512]
            for m in range(KC):
                wv, wk = load_w([128, KC, 128],
                                w_kvf_d.rearrange("(k p) n -> p k n", p=128)[:, :, m * 128:(m + 1) * 128])
                for (off, w) in ksubs:
                    pb, pk = bank()
                    for k in range(KC):
                        S.op("pe", lambda e, pb=pb, k=k: e.matmul(
                            pb[:, 0:w], wv[:, k, :], xn[:, k, off:off + w], start=(k == 0), stop=(k == KC - 1)),
                            r=wk + [("xn", k, off)], w=[pk])
                    ri, rk = head_norm_rs(pb, pk, w, EPS, 1.0 / HD)
                    ti = st_ring.next()
                    tk = ("st", ti)
                    S.op("dve", lambda e, pb=pb, ri=ri, ti=ti: e.scalar_tensor_tensor(
                        out=st_t[:, ti, 0:w], in0=pb[:, 0:w], scalar=ccol(C_GK), in1=rs_t[:, ri, 0:w],
                        op0=ALU.mult, op1=ALU.mult), r=[pk, rk, "consts"], w=[tk])
                    c0_ = tq0 + off - HALO
                    S.dma("sp", lambda e, ti=ti, m=m, c0_=c0_, w=w: e.dma_start(
                        out=Ks_d[m * 128:(m + 1) * 128, c0_:c0_ + w], in_=st_t[:, ti, 0:w]),
                        ("st", ti), r=[tk])
            cqi = ci_chunk % 2
            hoff = subs[0][1] if subs[0][1] != 512 else 0
            if hoff:
                S.op("dve", lambda e: e.memset(cq_t[:, cqi, 0:hoff], 0.0), w=[("cq", cqi, "halo")])
            tiles = [(off + i * 128, off) for (off, w) in ksubs for i in range(4)]
            for tt, (o, so) in enumerate(tiles):
                tok = tq0 + o - HALO
                jloc = tok // 128
                (V0, V0k), (V1, V1k), (Fp, Fk) = bank(), bank(), bank()
                for n, (pb, pk) in enumerate(((V0, V0k), (V1, V1k))):
                    for k in range(KC):
                        S.op("pe", lambda e, pb=pb, n=n, k=k: e.matmul(
                            pb[:, :], xn[:, k, o:o + 128], WV[:, k, n * 512:(n + 1) * 512],
                            start=(k == 0), stop=(k == KC - 1)), r=["WV", ("xn", k, so)], w=[pk])
                for k in range(KC):
                    S.op("pe", lambda e, k=k: e.matmul(
                        Fp[:, 0:NH], xn[:, k, o:o + 128], WV[:, k, D:D + NH], start=(k == 0), stop=(k == KC - 1)),
                        r=["WV", ("xn", k, so)], w=[Fk])
                vi = vst_ring.next()
                vk = ("vst", vi)
                S.op("act", lambda e, vi=vi: e.activation(out=vst_t[:, vi, 0:512], in_=V0[:, :], func=AF.Copy),
                     r=[V0k], w=[(vk, 0)])
                S.op("dve", lambda e, vi=vi: e.tensor_copy(out=vst_t[:, vi, 512:1024], in_=V1[:, :]),
                     r=[V1k], w=[(vk, 1)])
                S.dma("sp", lambda e, vi=vi, tok=tok: e.dma_start(
                    out=Vs_d[tok:tok + 128, :], in_=vst_t[:, vi, :]), ("vst", vi), r=[(vk, 0), (vk, 1)])
                zi = zf_ring.next()
                zk = ("zf", zi)
                S.op("dve", lambda e, zi=zi: e.tensor_tensor(
                    out=zf_t[:, zi, 0, :], in0=Fp[:, 0:NH], in1=consts[:, C_BF:C_BF + NH], op=ALU.add),
                    r=[Fk, "consts"], w=[(zk, 0)])
                S.op("act", lambda e, zi=zi: e.activation(
                    out=zf_t[:, zi, 1, :], in_=zf_t[:, zi, 0, :], func=AF.Exp, scale=-1.0), r=[(zk, 0)], w=[(zk, 1)])
                S.op("act", lambda e, zi=zi: e.activation(
                    out=zf_t[:, zi, 2, :], in_=zf_t[:, zi, 1, :], func=AF.Ln, bias=1.0, scale=1.0),
                    r=[(zk, 1)], w=[(zk, 2)])
                (Cs, Csk), (Ct, Ctk) = bank(), bank()
                S.op("pe", lambda e, zi=zi: e.matmul(Cs[:, 0:NH], tri_f, zf_t[:, zi, 2, :], start=True, stop=False),
                     r=[(zk, 2), "cmat"], w=[Csk])
                S.op("pe", lambda e: e.matmul(Cs[:, 0:NH], ones_f, sacc[:], start=False, stop=True),
                     r=["sacc", "cmat"], w=[Csk])
                S.op("pe", lambda e, zi=zi: e.matmul(Ct[0:NH, 0:128], zf_t[:, zi, 2, :], tri_f, start=True, stop=False),
                     r=[(zk, 2), "cmat"], w=[Ctk])
                S.op("pe", lambda e: e.matmul(Ct[0:NH, 0:128], sacc[:], ones_f, start=False, stop=True),
                     r=["sacc", "cmat"], w=[Ctk])
                S.op("dve", lambda e, jloc=jloc: e.tensor_copy(out=bias_all[:, NJo + jloc, :], in_=Cs[:, 0:NH]),
                     r=[Csk], w=[("bias", NJo + jloc)])
                S.op("act", lambda e, o=o: e.mul(out=cq_t[:, cqi, o:o + 128], in_=Ct[0:NH, 0:128], mul=-1.0),
                     r=[Ctk], w=[("cq", cqi, tt)])
                S.op("dve", lambda e, zi=zi: e.tensor_tensor(
                    out=sacc[:], in0=sacc[:], in1=zf_t[:, zi, 2, :], op=ALU.add), r=[(zk, 2), "sacc"], w=["sacc"])
            S.dma("sp", lambda e: e.dma_start(out=CQ_d[:, tq0:tq0 + L], in_=cq_t[:, cqi, 0:L]), ("cq", cqi),
                  r=[("cq", cqi, tt) for tt in range(len(tiles))] + ([("cq", cqi, "halo")] if hoff else []))

        def publish_and_gather():
            pb, pk = bank()
            S.op("pe", lambda e: e.matmul(pb[:, 0:NH], ones_f, sacc[:], start=True, stop=True),
                 r=["sacc", "cmat"], w=[pk])
            S.op("dve", lambda e: e.tensor_copy(out=totb[:], in_=pb[:, 0:NH]), r=[pk], w=["totb"])
            S.dma("sp", lambda e: e.dma_start(
                out=Dd_d[:, 0:NJo * NH], in_=bias_all[:, NJo:2 * NJo, :]), "pd0",
                r=[("bias", NJo + j) for j in range(NJo)])
            S.dma("sp", lambda e: e.dma_start(out=Dd_d[:, NJo * NH:ND], in_=totb[:]), "pd1", r=["totb"])
            S.barrier(lambda e: e.memset(scratch[:], 0.0))
            for name, src, dst, rows in (("ccK", Ks_d, gK_d, RK), ("ccV", Vs_d, gV_d, RV), ("ccD", Dd_d, gD_d, 128)):
                tot = src.shape[0]
                for k in range(tot // rows):
                    S.dma("pool", lambda e, src=src, dst=dst, k=k, rows=rows: e.collective_compute(
                        "AllGather", ALU.bypass, replica_groups=pairs, ins=[src[k * rows:(k + 1) * rows, :]],
                        outs=[dst[2 * k * rows:2 * (k + 1) * rows, :]]), name, inc=1)

        def copy_other():
            def fn(e, sem):
                par = e.partition_id() % 2
                for mine, src_half in ((0, 1), (1, 0)):
                    with e.If(par == mine):
                        for (o_d, g_d, rows) in ((oK_d, gK_d, RK), (oV_d, gV_d, RV), (oD_d, gD_d, 128)):
                            for k in range(o_d.shape[0] // rows):
                                b0 = (2 * k + src_half) * rows
                                e.dma_start(out=o_d[k * rows:(k + 1) * rows, :],
                                            in_=g_d[b0:b0 + rows, :]).then_inc(sem, 16)
            S.dma("sp", fn, "oth", w=["oth"], inc=16 * (D // RK + SO // RV + 1), self_inc=True)

        def other_bias():
            S.dma("sp", lambda e: e.dma_start(out=dpeer[:], in_=oD_d), "dpeer", r=["oth"], w=["dpeer"])
            for j in range(NJo):
                S.op("dve", lambda e, j=j: e.scalar_tensor_tensor(
                    out=bias_all[:, j, :], in0=dpeer[:, j * NH:(j + 1) * NH], scalar=ccol(C_PEN + j),
                    in1=dpeer[:, NJo * NH:ND], op0=ALU.add, op1=ALU.subtract),
                    r=["dpeer", "consts"], w=[("bias", j)])

        def qg_proj(j, tq0, subs, after_norm=None):
            rmsnorm(subs, C_GATT + (2 + j) * 8)
            if after_norm is not None:
                after_norm()
            wq2 = b_w_qg_d[j].rearrange("(k p) n -> p k n", p=128)
            for m in range(KC):
                wv, wk = load_w([128, KC, 128], wq2[:, :, m * 128:(m + 1) * 128])
                for (off, w) in subs:
                    pb, pk = bank()
                    for k in range(KC):
                        S.op("pe", lambda e, pb=pb, k=k: e.matmul(
                            pb[:, 0:w], wv[:, k, :], xn[:, k, off:off + w], start=(k == 0), stop=(k == KC - 1)),
                            r=wk + [("xn", k, off)], w=[pk])
                    ri, rk = head_norm_rs(pb, pk, w, HD * EPS, 1.0)
                    ti = st_ring.next()
                    tk = ("st", ti)
                    S.op("dve", lambda e, pb=pb, ri=ri, ti=ti: e.scalar_tensor_tensor(
                        out=st_t[:, ti, 0:w], in0=pb[:, 0:w], scalar=ccol(C_GQ + j), in1=rs_t[:, ri, 0:w],
                        op0=ALU.mult, op1=ALU.mult), r=[pk, rk, "consts"], w=[tk])
                    S.dma("sp", lambda e, ti=ti, m=m, off=off, w=w: e.dma_start(
                        out=Qs_d[m * 128:(m + 1) * 128, tq0 + off:tq0 + off + w], in_=st_t[:, ti, 0:w]),
                        ("st", ti), r=[tk])
            for m in range(KC):
                wv, wk = load_w([128, KC, 128], wq2[:, :, D + m * 128:D + (m + 1) * 128])
                for (off, w) in subs:
                    pb, pk = bank()
                    for k in range(KC):
                        S.op("pe", lambda e, pb=pb, k=k: e.matmul(
                            pb[:, 0:w], wv[:, k, :], xn[:, k, off:off + w], start=(k == 0), stop=(k == KC - 1)),
                            r=wk + [("xn", k, off)], w=[pk])
                    ti = st_ring.next()
                    tk = ("st", ti)
                    S.op("act", lambda e, pb=pb, ti=ti: e.activation(
                        out=st_t[:, ti, 0:w], in_=pb[:, 0:w], func=AF.Sigmoid), r=[pk], w=[tk])
                    S.dma("sp", lambda e, ti=ti, m=m, off=off, w=w: e.dma_start(
                        out=Gs_d[m * 128:(m + 1) * 128, tq0 + off:tq0 + off + w], in_=st_t[:, ti, 0:w]),
                        ("st", ti), r=[tk])

        def attention():
            Kv = [AA[:, s * SV:(s + 1) * SV] for s in range(2)]
            Pt = AA[:, 2 * SV:2 * SV + 4096].rearrange("p (n t) -> p n t", n=8)
            AXb = AX[:, :].bitcast(BF16)
            Vv = [AXb[:, s * 8192: s * 8192 + NJ * 128].rearrange("p (j d) -> p j d", d=128) for s in range(2)]
            Qv = [AN[:, s * QN:(s + 1) * QN] for s in range(2)]
            Gv = [AW[:, s * QN:(s + 1) * QN] for s in range(2)]
            p_ring = Ring("p", 8)
            s_banks = [0, 1, 2, 3]
            o_banks = [4, 5]
            sb_ring = Ring("sb", 4)
            ob_ring = Ring("ob", 2)
            og_ring = Ring("og", 2)
            for s in range(2):
                S.op("dve", lambda e, s=s: e.memset(Kv[s][64:65, :], 1.0), w=[("K1", s)])
                S.op("dve", lambda e, s=s: e.memset(Vv[s][:, :, 64:128], 1.0), w=[("V1", s)])
            blocks = [(0, HALO)] + [(HALO + 512 * i, 512) for i in range(SO // 512)]
            def head_loads(h):
                s = h % 2
                kK, kV, kQ, kG = ("K", s), ("V", s), ("Q", s), ("G", s)
                S.dma("sp", lambda e: e.dma_start(out=Kv[s][0:64, 0:SO], in_=oK_d[h * 64:(h + 1) * 64, :]),
                      ("Ko", s), r=["oth"], w=[(kK, 0)])
                S.dma("sp", lambda e: e.dma_start(out=Kv[s][0:64, SO:2 * SO], in_=Ks_d[h * 64:(h + 1) * 64, :]),
                      ("K", s), w=[(kK, 1)])
                S.dma("sp", lambda e: e.dma_start(
                    out=Vv[s][:, 0:NJo, 0:64],
                    in_=oV_d[:, h * 64:(h + 1) * 64].rearrange("(j p) d -> p j d", p=128)),
                    ("Vo", s), r=["oth"], w=[(kV, 0)])
                S.dma("sp", lambda e: e.dma_start(
                    out=Vv[s][:, NJo:2 * NJo, 0:64],
                    in_=Vs_d[:, h * 64:(h + 1) * 64].rearrange("(j p) d -> p j d", p=128)),
                    ("V", s), w=[(kV, 1)])
                S.dma("sp", lambda e: e.dma_start(out=Qv[s][0:64, :], in_=Qs_d[h * 64:(h + 1) * 64, :]),
                      ("Q", s), w=[(kQ, 0)])
                S.dma("sp", lambda e: e.dma_start(out=Qv[s][64:65, :], in_=CQ_d[h:h + 1, :]),
                      ("Q2", s), w=[(kQ, 1)])
                S.dma("sp", lambda e: e.dma_start(out=Gv[s][0:64, :], in_=Gs_d[h * 64:(h + 1) * 64, :]),
                      ("G", s), w=[kG])

            items = []
            for h in range(NH):
                for (tq0, w) in blocks:
                    qa0 = Q0 + tq0
                    jlist = []
                    for j in range(NJ):
                        if j * 128 > qa0 + w - 1:
                            break
                        if j * 128 + 127 <= qa0:
                            jlist.append((j, 0, None))
                        else:
                            offs = qa0 - j * 128
                            jlist.append((j, max(0, -offs), offs + 384))
                    for n, (j, cs, c0) in enumerate(jlist):
                        items.append(dict(h=h, tq0=tq0, w=w, j=j, cs=cs, c0=c0, first=(n == 0),
                                          last=(n == len(jlist) - 1), newhead=(n == 0 and tq0 == 0)))

            def qk(it):
                h, tq0, w, j, cs = it["h"], it["tq0"], it["w"], it["j"], it["cs"]
                s = h % 2
                si = s_banks[sb_ring.next()]
                Sb, Sbk = banks[si], ("ps", si)
                it["Sb"], it["Sbk"] = Sb, Sbk
                S.op("pe", lambda e: e.matmul(
                    Sb[:, cs:w], Kv[s][0:65, j * 128:(j + 1) * 128], Qv[s][0:65, tq0 + cs:tq0 + w],
                    start=True, stop=True), r=[(("K", s), 0), (("K", s), 1), ("K1", s), (("Q", s), 0), (("Q", s), 1)], w=[Sbk])

            cur = {}

            def rest(it):
                h, tq0, w, j, cs, c0 = it["h"], it["tq0"], it["w"], it["j"], it["cs"], it["c0"]
                s = h % 2
                if it["newhead"] and h + 1 < NH:
                    head_loads(h + 1)
                Sb, Sbk = it["Sb"], it["Sbk"]
                kV, kG = ("V", s), ("G", s)
                if it["first"]:
                    oi = o_banks[ob_ring.next()]
                    cur["Ob"], cur["Obk"] = banks[oi], ("ps", oi)
                Ob, Obk = cur["Ob"], cur["Obk"]
                pi = p_ring.next()
                pk_ = ("P", pi)
                S.op("act", lambda e: e.activation(
                    out=Pt[:, pi, cs:w], in_=Sb[:, cs:w], func=AF.Exp, bias=bias_all[:, j, h:h + 1], scale=1.0),
                    r=[Sbk, ("bias", j)], w=[pk_])
                if c0 is not None:
                    S.op("dve", lambda e: e.tensor_tensor(
                        out=Pt[:, pi, cs:w], in0=Pt[:, pi, cs:w], in1=mask_bf[:, c0 + cs:c0 + w], op=ALU.mult),
                        r=[pk_, "mask_bf"], w=[pk_])
                first, last = it["first"], it["last"]
                S.op("pe", lambda e: e.matmul(
                    Ob[:, cs:w], Vv[s][:, j, :], Pt[:, pi, cs:w], start=first, stop=last),
                    r=[(kV, 0), (kV, 1), ("V1", s), pk_], w=[Obk])
                if not last:
                    return
                gi = og_ring.next()
                S.op("dve", lambda e: e.tensor_scalar(
                    out=rd_t[:, gi, 0:w], in0=Ob[64:128, 0:w], scalar1=1e-30, scalar2=None, op0=ALU.add),
                    r=[Obk], w=[("rd", gi)])
                S.op("dve", lambda e: e.reciprocal(out=rd_t[:, gi, 0:w], in_=rd_t[:, gi, 0:w]),
                     r=[("rd", gi)], w=[("rd", gi)])
                S.op("dve", lambda e: e.tensor_tensor(
                    out=on_t[:, gi, 0:w], in0=Ob[0:64, 0:w], in1=rd_t[:, gi, 0:w], op=ALU.mult),
                    r=[Obk, ("rd", gi)], w=[("on", gi)])
                ti = st_ring.next()
                tk = ("st", ti)
                if tq0 == 0:
                    S.op("dve", lambda e: e.scalar_tensor_tensor(
                        out=st_t[0:64, ti, 0:w], in0=on_t[:, gi, 0:w], scalar=consts[0:64, C_HS:C_HS + 1],
                        in1=Gv[s][0:64, tq0:tq0 + w], op0=ALU.mult, op1=ALU.mult),
                        r=[("on", gi), kG, "consts"], w=[tk])
                else:
                    S.op("dve", lambda e: e.tensor_tensor(
                        out=st_t[0:64, ti, 0:w], in0=on_t[:, gi, 0:w], in1=Gv[s][0:64, tq0:tq0 + w], op=ALU.mult),
                        r=[("on", gi), kG], w=[tk])
                S.dma("sp", lambda e: e.dma_start(
                    out=Os_d[h * 64:(h + 1) * 64, tq0:tq0 + w], in_=st_t[0:64, ti, 0:w]), ("st", ti), r=[tk])

            head_loads(0)
            LA = 3
            for n in range(min(LA, len(items))):
                qk(items[n])
            for n in range(len(items)):
                if n + LA < len(items):
                    qk(items[n + LA])
                rest(items[n])

        def load_og(tq0, L, subs):
            for c in range(KC):
                S.dma("sp", lambda e, c=c: e.dma_start(
                    out=act[:, c, 0:L], in_=Os_d[c * 128:(c + 1) * 128, tq0:tq0 + L]), ("ogl", c),
                    w=[("act", c, off) for off in ALLOFF])

        own = []
        tq = 0
        for i in range(SO // TC):
            L = TC + (HALO if i == 0 else 0)
            subs = ([(0, HALO)] if i == 0 else []) + [((HALO if i == 0 else 0) + o, 512) for o in range(0, TC, 512)]
            own.append((tq, L, subs))
            tq += L

        def nxt(i, fn):
            if i + 1 >= len(own):
                return None
            t_, L_, s_ = own[i + 1]
            return lambda: fn(t_, L_, s_)

        load_x2(xT_d, own[0][0], own[0][1], own[0][2])
        for ci_, (tq0, L, subs) in enumerate(own):
            for l in range(2):
                mixer(l, subs, L)
                ffn(l, subs, L)
            store_x(Xs_d, tq0, L, subs)
            kv_proj(tq0, subs, L, ci_, after_norm=nxt(ci_, lambda t_, L_, s_: load_x2(xT_d, t_, L_, s_)))
        publish_and_gather()

        load_x2(Xs_d, own[0][0], own[0][1], own[0][2])
        for ci_, (tq0, L, subs) in enumerate(own):
            qg_proj(0, tq0, subs, after_norm=nxt(ci_, lambda t_, L_, s_: load_x2(Xs_d, t_, L_, s_)))
        S.barrier(lambda e: e.memset(scratch[:], 0.0))
        copy_other()
        other_bias()
        attention()
        S.barrier(lambda e: e.memset(scratch[:], 0.0))
        def ld23(t_, L_, s_):
            load_x2(Xs_d, t_, L_, s_)
            load_og(t_, L_, s_)

        ld23(*own[0])
        for ci_, (tq0, L, subs) in enumerate(own):
            out_proj(b_w_out_d[0], subs)
            ffn(2, subs, L)
            store_x(Xs_d, tq0, L, subs)
            qg_proj(1, tq0, subs, after_norm=nxt(ci_, ld23))
        S.barrier(lambda e: e.memset(scratch[:], 0.0))
        attention()
        S.barrier(lambda e: e.memset(scratch[:], 0.0))
        for i, (tq0, L, subs) in enumerate(own):
            load_x2(Xs_d, tq0, L, subs)
            load_og(tq0, L, subs)
            out_proj(b_w_out_d[1], subs)
            ffn(3, subs, L)
            if i == 0:
                store_x(out_d, 0, L - HALO, subs, src_off=HALO)
            else:
                store_x(out_d, tq0 - HALO, L, subs)

        S.emit(nc, es)
    return nc


def host_consts(SV, SO, half, attn_norm, ffn_norm, a_conv, kv_norm, b_f, k_norm, q_norm, ffn_conv):
    NJo = SO // 128
    c = np.zeros((128, C_PEN + NJo), np.float32)
    c[:, C_GATT:C_GATT + 32] = attn_norm.reshape(4, 8, 128).transpose(2, 0, 1).reshape(128, 32)
    c[:, C_GFFN:C_GFFN + 32] = ffn_norm.reshape(4, 8, 128).transpose(2, 0, 1).reshape(128, 32)
    c[:, C_GKV:C_GKV + 8] = kv_norm.reshape(8, 128).T
    c[:, C_ACONV:C_ACONV + 48] = a_conv.reshape(2, 3, 8, 128).transpose(3, 0, 1, 2).reshape(128, 48)
    c[:, C_FCONV:C_FCONV + 264] = ffn_conv.reshape(4, 3, 22, 128).transpose(3, 0, 1, 2).reshape(128, 264)
    c[:, C_GK] = np.tile(k_norm, 2)
    c[:, C_GQ] = np.tile(q_norm[0], 2)
    c[:, C_GQ + 1] = np.tile(q_norm[1], 2)
    c[:, C_HS] = 1.0 if half == 1 else 0.0
    c[:, C_BF:C_BF + NH] = np.broadcast_to(b_f[None, :], (128, NH))
    if half == 0:
        c[:, C_PEN:] = np.float32(PEN)
    return c


def host_cmat():
    m = np.zeros((128, 384 + 896), np.float32)
    m[:, 0:128] = 1.0
    m[0:64, 128:192] = 1.0
    m[64:128, 192:256] = 1.0
    m[:, 256:384] = np.triu(np.ones((128, 128), np.float32))
    k = np.arange(128)[:, None]
    cc = np.arange(896)[None, :]
    m[:, 384:] = (k <= cc - 384).astype(np.float32)
    return m


def make_in_maps(SV, SO, x_halves, attn_norm, ffn_norm, a_w_in, a_conv, a_w_out, kv_norm, w_kvf, b_f, k_norm,
                 b_w_qg, q_norm, b_w_out, ffn_w_up, ffn_conv, ffn_w_down):
    cm = host_cmat()
    f = lambda a: np.ascontiguousarray(np.asarray(a, dtype=np.float32))
    shared = {"cmat": cm, "a_w_in": f(a_w_in), "a_w_out": f(a_w_out), "w_kvf": f(w_kvf), "b_w_qg": f(b_w_qg),
              "b_w_out": f(b_w_out), "ffn_w_up": f(ffn_w_up), "ffn_w_down": f(ffn_w_down)}
    maps = []
    for half, xT in x_halves:
        m = dict(shared)
        m["xT"] = np.ascontiguousarray(xT, dtype=np.float32)
        m["consts"] = host_consts(SV, SO, half, f(attn_norm), f(ffn_norm), f(a_conv), f(kv_norm), f(b_f), f(k_norm),
                                  f(q_norm), f(ffn_conv))
        maps.append(m)
    return maps


_NC_CACHE = {}


def kernel(x, attn_norm, ffn_norm, a_w_in, a_conv, a_w_out, kv_norm, w_kvf, b_f, k_norm, b_w_qg, q_norm, b_w_out,
           ffn_w_up, ffn_conv, ffn_w_down):
    x = np.asarray(x, dtype=np.float32)
    B, SEQ, _ = x.shape
    SO = SEQ // 2
    SV = SEQ
    TC = 1024
    halves = []
    for b in range(B):
        for half in range(2):
            if half == 1:
                xT = x[b, SO - HALO:].T
            else:
                xT = np.concatenate([np.zeros((D, HALO), np.float32), x[b, :SO].T], axis=1)
            halves.append((half, xT))
    in_maps = make_in_maps(SV, SO, halves, attn_norm, ffn_norm, a_w_in, a_conv, a_w_out, kv_norm, w_kvf, b_f, k_norm,
                           b_w_qg, q_norm, b_w_out, ffn_w_up, ffn_conv, ffn_w_down)
    key = (SV, SO, TC)
    if key not in _NC_CACHE:
        _NC_CACHE[key] = build_nc(SV, SO, TC, n_cores=len(in_maps))
    nc = _NC_CACHE[key]
    res = run_bass_kernel_spmd(nc, in_maps, core_ids=list(range(len(in_maps))))
    out = np.empty((B, SEQ, D), np.float32)
    for i, r in enumerate(res.results):
        b, half = divmod(i, 2)
        out[b, half * SO:(half + 1) * SO, :] = r["outT"].T
    return out
```

```python
import types
import numpy as np
from contextlib import ExitStack
import concourse.bass as bass
import concourse.mybir as mybir
from concourse.bass_utils import run_bass_kernel_spmd

F32 = mybir.dt.float32
BF16 = mybir.dt.bfloat16
AF = mybir.ActivationFunctionType
ALU = mybir.AluOpType

D = 1024
KC = 8
DFF = 2816
FC = 22
NH = 16
HD = 64
HALO = 32
EPS = 1e-6
PEN = -30000.0

C_GATT = 0
C_GFFN = 32
C_GKV = 64
C_ACONV = 72
C_FCONV = 120
C_GK = 384
C_GQ = 385
C_HS = 387
C_BF = 388
C_PEN = 404


def _freeze(fn):
    if fn.__closure__ is None:
        return fn
    cells = []
    for c in fn.__closure__:
        try:
            cells.append(types.CellType(c.cell_contents))
        except ValueError:
            cells.append(c)
    return types.FunctionType(fn.__code__, fn.__globals__, fn.__name__, fn.__defaults__, tuple(cells))


class Op:
    __slots__ = ("eng", "fn", "deps", "dma", "sem_key", "count", "signal", "idx", "inc", "self_inc")


class Sched:
    ENGS = ("pe", "act", "dve", "pool", "sp")

    def __init__(self):
        self.ops = []
        self.last_writer = {}
        self.readers = {}
        self.barrier_op = None
        self.last_on_eng = {}
        self.last_dma_on_sem = {}
        self.groups = []

    def _add(self, eng, fn, r, w, dma, sem_key):
        op = Op()
        op.eng, op.fn, op.dma, op.sem_key = eng, _freeze(fn), dma, sem_key
        op.idx = len(self.ops)
        op.signal = dma
        op.count = 0
        raw = set()
        other = set()
        if self.barrier_op is not None:
            raw.add(self.barrier_op)
        for k in r:
            lw = self.last_writer.get(k)
            if lw is not None:
                raw.add(lw)
        for k in w:
            lw = self.last_writer.get(k)
            if lw is not None:
                other.add(lw)
            for rd in self.readers.get(k, ()):
                other.add(rd)
        for k in r:
            self.readers.setdefault(k, []).append(op.idx)
        for k in w:
            self.last_writer[k] = op.idx
            self.readers[k] = []
        deps = []
        for d in raw | other:
            p = self.ops[d]
            if (not p.dma) and (not dma) and p.eng == eng:
                if eng == "pe":
                    continue
                if d not in raw:
                    continue
            deps.append(d)
        op.deps = deps
        self.ops.append(op)
        if dma:
            self.last_dma_on_sem[sem_key] = op.idx
        else:
            self.last_on_eng[eng] = op.idx
        return op.idx

    def op(self, eng, fn, r=(), w=()):
        return self._add(eng, fn, r, w, False, None)

    def dma(self, eng, fn, sem_key, r=(), w=(), inc=16, self_inc=False):
        i = self._add(eng, fn, r, w, True, sem_key)
        self.ops[i].inc = inc
        self.ops[i].self_inc = self_inc
        return i

    def barrier(self, fn):
        deps = set(self.last_on_eng.values()) | set(self.last_dma_on_sem.values())
        op = Op()
        op.eng, op.fn, op.dma, op.sem_key = "dve", _freeze(fn), False, None
        op.idx = len(self.ops)
        op.signal = False
        op.count = 0
        op.deps = sorted(deps)
        self.ops.append(op)
        self.last_on_eng["dve"] = op.idx
        self.barrier_op = op.idx
        self.last_writer = {}
        self.readers = {}
        return op.idx

    def emit(self, nc, es):
        ops = self.ops
        for op in ops:
            for d in op.deps:
                ops[d].signal = True
        eng_sem = {e: es.enter_context(nc.semaphore("s_" + e)) for e in ("pe", "act", "dve", "pool")}
        dma_sems = {}
        cnt = {e: 0 for e in eng_sem}
        dcnt = {}
        for op in ops:
            if op.dma:
                if op.sem_key not in dma_sems:
                    dma_sems[op.sem_key] = es.enter_context(nc.semaphore("d_%d" % len(dma_sems)))
                    dcnt[op.sem_key] = 0
                dcnt[op.sem_key] += op.inc
                op.count = dcnt[op.sem_key]
            elif op.signal:
                cnt[op.eng] += 1
                op.count = cnt[op.eng]
        for grp in self.groups:
            c = max(ops[i].count for i in grp)
            for i in grp:
                ops[i].count = c
        per_eng = {e: [] for e in self.ENGS}
        for op in ops:
            per_eng[op.eng].append(op)
        final = []
        for k, s in dma_sems.items():
            final.append((s, dcnt[k]))
        for e, s in eng_sem.items():
            if cnt[e]:
                final.append((s, cnt[e]))
        block = es.enter_context(nc.Block())

        def stream(e, handle):
            seen = {}
            for op in per_eng[e]:
                waits = {}
                for d in op.deps:
                    p = ops[d]
                    s = dma_sems[p.sem_key] if p.dma else eng_sem[p.eng]
                    key = id(s)
                    if seen.get(key, 0) >= p.count:
                        continue
                    if key not in waits or waits[key][1] < p.count:
                        waits[key] = (s, p.count)
                for key, (s, c) in waits.items():
                    handle.wait_ge(s, c)
                    seen[key] = c
                if op.dma and op.self_inc:
                    op.fn(handle, dma_sems[op.sem_key])
                    continue
                ins = op.fn(handle)
                if op.dma:
                    ins.then_inc(dma_sems[op.sem_key], op.inc)
                elif op.signal:
                    ins.then_inc(eng_sem[op.eng], 1)
            if e == "sp":
                for s, c in final:
                    handle.wait_ge(s, c)

        @block.tensor
        def _(h):
            stream("pe", h)

        @block.scalar
        def _(h):
            stream("act", h)

        @block.vector
        def _(h):
            stream("dve", h)

        @block.gpsimd
        def _(h):
            stream("pool", h)

        @block.sync
        def _(h):
            stream("sp", h)


class Ring:
    def __init__(self, name, n):
        self.name, self.n, self.i = name, n, -1

    def next(self):
        self.i = (self.i + 1) % self.n
        return self.i


def build_nc(SV, SO, TC, debug=False, n_cores=8):
    assert SV % 512 == 0 and SO % 512 == 0 and TC % 512 == 0 and SO % TC == 0 and SV % TC == 0
    NJ = SV // 128
    QN = SO + HALO
    Q0 = SV - QN
    TB = TC + HALO
    NJo = SO // 128
    assert SV == 2 * SO
    NCONST = C_PEN + NJo
    ND = (NJo + 1) * NH
    pairs = [[2 * i, 2 * i + 1] for i in range(n_cores // 2)]
    RK = min(D, (2 << 20) // (SO * 2))
    RV = min(SO, 1024)
    nc = bass.Bass("TRN2", target_bir_lowering=False)
    S = Sched()

    def din(name, shape, dt=F32):
        return nc.dram_tensor(name, list(shape), dt, kind="ExternalInput").ap()

    def dscr(name, shape, dt):
        if debug:
            return nc.dram_tensor(name, list(shape), dt, kind="ExternalOutput").ap()
        return nc.dram_tensor(name, list(shape), dt).ap()

    xT_d = din("xT", [D, QN])
    consts_d = din("consts", [128, NCONST])
    cmat_d = din("cmat", [128, 384 + 896])
    a_w_in_d = din("a_w_in", [2, D, 3 * D])
    a_w_out_d = din("a_w_out", [2, D, D])
    w_kvf_d = din("w_kvf", [D, 2 * D + NH])
    b_w_qg_d = din("b_w_qg", [2, D, 2 * D])
    b_w_out_d = din("b_w_out", [2, D, D])
    ffn_w_up_d = din("ffn_w_up", [4, D, 2 * DFF])
    ffn_w_down_d = din("ffn_w_down", [4, DFF, D])
    out_d = nc.dram_tensor("outT", [D, SO], F32, kind="ExternalOutput").ap()

    def dcc(name, shape, dt):
        return nc.dram_tensor(name, list(shape), dt).ap()

    Ks_d = dcc("pubK", [D, SO], BF16)
    Vs_d = dcc("pubV", [SO, D], BF16)
    Dd_d = dcc("pubD", [128, ND], F32)
    gK_d = dcc("gathK", [2 * D, SO], BF16)
    gV_d = dcc("gathV", [2 * SO, D], BF16)
    gD_d = dcc("gathD", [256, ND], F32)
    oK_d = dcc("othK", [D, SO], BF16)
    oV_d = dcc("othV", [SO, D], BF16)
    oD_d = dcc("othD", [128, ND], F32)
    CQ_d = dscr("CQ", [NH, QN], BF16)
    Xs_d = dscr("Xs", [D, QN], F32)
    Qs_d = dscr("Qs", [D, QN], BF16)
    Gs_d = dscr("Gs", [D, QN], BF16)
    Os_d = dscr("Os", [D, QN], BF16)

    with ExitStack() as es:
        def sb(name, shape, dt):
            return es.enter_context(nc.sbuf_tensor("sb_" + name, list(shape), dt))

        AX = sb("AX", [128, max(KC * TB, 8192)], F32)
        AA = sb("AA", [128, max(FC * TB, 2 * SV + 4096)], BF16)
        AN = sb("AN", [128, max(KC * TB, 2 * QN)], BF16)
        AW = sb("AW", [128, max(4 * 3072, 2 * QN)], BF16)
        WV = sb("WV", [128, KC, D + NH], BF16)
        consts = sb("consts", [128, NCONST], F32)
        cmat = sb("cmat", [128, 384 + 896], F32)
        ones_bf = sb("ones_bf", [128, 128], BF16)
        bd_bf = sb("bd_bf", [128, 128], BF16)
        mask_bf = sb("mask_bf", [128, 896], BF16)
        bias_all = sb("bias_all", [128, NJ, NH], F32)
        sacc = sb("sacc", [128, NH], F32)
        carryA = sb("carryA", [128, 2, KC, 2], F32)
        carryF = sb("carryF", [128, 4, FC, 2], F32)
        sq_t = sb("sq_t", [128, 3, 512], BF16)
        rs_t = sb("rs_t", [128, 2, 512], F32)
        csb_t = sb("csb_t", [128, 2, 512], F32)
        cv_t = sb("cv_t", [128, 3, 2 + TB], F32)
        u_t = sb("u_t", [128, 2, 512], F32)
        s_t = sb("s_t", [128, 2, 512], F32)
        st_t = sb("st_t", [128, 4, 512], BF16)
        vst_t = sb("vst_t", [128, 2, D], BF16)
        zf_t = sb("zf_t", [128, 2, 3, NH], F32)
        cq_t = sb("cq_t", [NH, 2, TB], BF16)
        dpeer = sb("dpeer", [128, ND], F32)
        totb = sb("totb", [128, NH], F32)
        on_t = sb("on_t", [64, 2, 512], F32)
        rd_t = sb("rd_t", [64, 2, 512], F32)
        scratch = sb("scratch", [128, 8], F32)
        banks = [es.enter_context(nc.psum_tensor("ps%d" % i, [128, 512], F32)) for i in range(8)]

        xT = AX[:, 0:KC * TB].rearrange("p (c t) -> p c t", c=KC)
        xn = AN[:, 0:KC * TB].rearrange("p (c t) -> p c t", c=KC)
        act = AA[:, 0:FC * TB].rearrange("p (c t) -> p c t", c=FC)
        ones_f = cmat[:, 0:128]
        tri_f = cmat[:, 256:384]

        def ccol(i):
            return consts[:, i:i + 1]

        ps_ring = Ring("ps", 8)
        sq_ring = Ring("sq", 3)
        rs_ring = Ring("rs", 2)
        csb_ring = Ring("csb", 2)
        cv_ring = Ring("cv", 3)
        u_ring = Ring("u", 2)
        s_ring = Ring("s", 2)
        st_ring = Ring("st", 4)
        vst_ring = Ring("vst", 2)
        zf_ring = Ring("zf", 2)
        w_ring = Ring("w", 4)

        def bank():
            i = ps_ring.next()
            return banks[i], ("ps", i)

        S.dma("sp", lambda e: e.dma_start(out=consts[:], in_=consts_d), "c0", w=["consts"])
        S.dma("sp", lambda e: e.dma_start(out=cmat[:], in_=cmat_d), "c1", w=["cmat"])
        S.op("dve", lambda e: e.tensor_copy(out=ones_bf[:], in_=cmat[:, 0:128]), r=["cmat"], w=["ones_bf"])
        S.op("dve", lambda e: e.tensor_copy(out=bd_bf[:], in_=cmat[:, 128:256]), r=["cmat"], w=["bd_bf"])
        S.op("dve", lambda e: e.tensor_copy(out=mask_bf[:], in_=cmat[:, 384:384 + 896]), r=["cmat"], w=["mask_bf"])
        S.op("dve", lambda e: e.memset(sacc[:], 0.0), w=["sacc"])
        S.op("dve", lambda e: e.memset(carryA[:], 0.0), w=["carryA"])
        S.op("dve", lambda e: e.memset(carryF[:], 0.0), w=["carryF"])
        S.dma("pool", lambda e: e.dma_start(
            out=WV[:], in_=w_kvf_d[:, D:2 * D + NH].rearrange("(k p) n -> p k n", p=128)), "wv", w=["WV"])
        S.barrier(lambda e: e.memset(scratch[:], 0.0))

        def load_w(view_shape, src_ap):
            slot = w_ring.next()
            n = 1
            for d_ in view_shape[1:]:
                n *= d_
            flat = AW[:, slot * 3072: slot * 3072 + n]
            if len(view_shape) == 3:
                view = flat.rearrange("p (k n) -> p k n", k=view_shape[1])
            else:
                view = flat.rearrange("p (k g n) -> p k g n", k=view_shape[1], g=view_shape[2])
            keys = [("w", slot, 0), ("w", slot, 1), ("w", slot, 2)]
            if len(view_shape) == 3:
                S.dma("pool", lambda e: e.dma_start(out=view, in_=src_ap), ("w", slot), w=keys)
            else:
                ng = view_shape[2]
                grp = []
                for g in range(ng):
                    wk_ = [keys[g]] + ([keys[2]] if (ng == 2 and g == 1) else [])
                    grp.append(S.dma("pool", lambda e: e.dma_start(out=view[:, :, g, :], in_=src_ap[:, :, g, :]),
                                     ("w", slot), w=wk_))
                S.groups.append(grp)
            return view, keys

        def slab_cols(w2d, ngroups, gsize, c0, width=128):
            v = w2d.rearrange("(k p) (g c) -> p k g c", p=128, g=ngroups)
            return v[:, :, :, c0:c0 + width]

        def rmsnorm(subs, gbase):
            for (off, w) in subs:
                pb, pk = bank()
                for c in range(KC):
                    si = sq_ring.next()
                    sk = ("sq", si)
                    S.op("act", lambda e, c=c, si=si: e.activation(
                        out=sq_t[:, si, 0:w], in_=xT[:, c, off:off + w], func=AF.Square),
                        r=[("x", c, off)], w=[sk])
                    S.op("pe", lambda e, c=c, si=si: e.matmul(
                        pb[:, 0:w], ones_bf[:], sq_t[:, si, 0:w], start=(c == 0), stop=(c == KC - 1)),
                        r=[sk, "ones_bf"], w=[pk])
                ri = rs_ring.next()
                rk = ("rs", ri)
                S.op("act", lambda e, ri=ri: e.activation(
                    out=rs_t[:, ri, 0:w], in_=pb[:, 0:w], func=AF.Sqrt, bias=EPS, scale=1.0 / D),
                    r=[pk], w=[rk])
                S.op("dve", lambda e, ri=ri: e.reciprocal(out=rs_t[:, ri, 0:w], in_=rs_t[:, ri, 0:w]),
                     r=[rk], w=[rk])
                for c in range(KC):
                    S.op("dve", lambda e, c=c, ri=ri: e.scalar_tensor_tensor(
                        out=xn[:, c, off:off + w], in0=xT[:, c, off:off + w], scalar=ccol(gbase + c),
                        in1=rs_t[:, ri, 0:w], op0=ALU.mult, op1=ALU.mult),
                        r=[("x", c, off), rk, "consts"], w=[("xn", c, off)])

        def head_norm_rs(pb, pk, w, bias_eps, scale):
            si = sq_ring.next()
            sk = ("sq", si)
            S.op("act", lambda e: e.activation(out=sq_t[:, si, 0:w], in_=pb[:, 0:w], func=AF.Square),
                 r=[pk], w=[sk])
            sb_, sbk = bank()
            S.op("pe", lambda e: e.matmul(sb_[:, 0:w], bd_bf[:], sq_t[:, si, 0:w], start=True, stop=True),
                 r=[sk, "bd_bf"], w=[sbk])
            ri = rs_ring.next()
            rk = ("rs", ri)
            S.op("act", lambda e: e.activation(out=rs_t[:, ri, 0:w], in_=sb_[:, 0:w], func=AF.Sqrt,
                                               bias=bias_eps, scale=scale), r=[sbk], w=[rk])
            S.op("dve", lambda e: e.reciprocal(out=rs_t[:, ri, 0:w], in_=rs_t[:, ri, 0:w]), r=[rk], w=[rk])
            return ri, rk

        def conv_head_init(ci, carry_ap, ckey):
            S.op("act", lambda e: e.activation(out=cv_t[:, ci, 0:2], in_=carry_ap, func=AF.Copy),
                 r=[ckey], w=[("cv", ci, "h")])

        def conv_tail_save(ci, carry_ap, ckey, L, lastoff):
            S.op("act", lambda e: e.activation(out=carry_ap, in_=cv_t[:, ci, L:L + 2], func=AF.Copy),
                 r=[("cv", ci, lastoff)], w=[ckey])

        def mixer(l, subs, L):
            rmsnorm(subs, C_GATT + l * 8)
            for j in range(KC):
                wv, wk = load_w([128, KC, 3, 128], slab_cols(a_w_in_d[l], 3, D, j * 128))
                ci = cv_ring.next()
                ck = ("cA", l, j)
                conv_head_init(ci, carryA[:, l, j, :], ck)
                prev = "h"
                for (off, w) in subs:
                    pbs = [bank() for _ in range(3)]
                    for g in range(3):
                        pb, pk = pbs[g]
                        for k in range(KC):
                            S.op("pe", lambda e, pb=pb, g=g, k=k: e.matmul(
                                pb[:, 0:w], wv[:, k, g, :], xn[:, k, off:off + w], start=(k == 0), stop=(k == KC - 1)),
                                r=wk + [("xn", k, off)], w=[pk])
                    (Bp, Bk), (Cp, Ck), (Hp, Hk) = pbs
                    qi = csb_ring.next()
                    qk = ("csb", qi)
                    S.op("act", lambda e, qi=qi, Cp=Cp: e.activation(out=csb_t[:, qi, 0:w], in_=Cp[:, 0:w], func=AF.Copy),
                         r=[Ck], w=[qk])
                    S.op("dve", lambda e, qi=qi, Hp=Hp: e.tensor_tensor(
                        out=cv_t[:, ci, 2 + off:2 + off + w], in0=Hp[:, 0:w], in1=csb_t[:, qi, 0:w], op=ALU.mult),
                        r=[Hk, qk], w=[("cv", ci, off)])
                    ui = u_ring.next()
                    uk = ("u", ui)
                    base = C_ACONV + l * 24
                    S.op("dve", lambda e, ui=ui: e.tensor_scalar(
                        out=u_t[:, ui, 0:w], in0=cv_t[:, ci, off:off + w], scalar1=ccol(base + j), scalar2=None,
                        op0=ALU.mult), r=[("cv", ci, off), ("cv", ci, prev), "consts"], w=[uk])
                    S.op("dve", lambda e, ui=ui: e.scalar_tensor_tensor(
                        out=u_t[:, ui, 0:w], in0=cv_t[:, ci, off + 1:off + 1 + w], scalar=ccol(base + 8 + j),
                        in1=u_t[:, ui, 0:w], op0=ALU.mult, op1=ALU.add),
                        r=[("cv", ci, off), ("cv", ci, prev), uk], w=[uk])
                    S.op("dve", lambda e, ui=ui: e.scalar_tensor_tensor(
                        out=u_t[:, ui, 0:w], in0=cv_t[:, ci, off + 2:off + 2 + w], scalar=ccol(base + 16 + j),
                        in1=u_t[:, ui, 0:w], op0=ALU.mult, op1=ALU.add),
                        r=[("cv", ci, off), uk], w=[uk])
                    S.op("dve", lambda e, ui=ui, Bp=Bp: e.tensor_tensor(
                        out=act[:, j, off:off + w], in0=Bp[:, 0:w], in1=u_t[:, ui, 0:w], op=ALU.mult),
                        r=[Bk, uk], w=[("act", j, off)])
                    prev = off
                conv_tail_save(ci, carryA[:, l, j, :], ck, L, prev)
            out_proj(a_w_out_d[l], subs)

        def out_proj(w2d, subs):
            for m in range(KC):
                wv, wk = load_w([128, KC, 128], w2d.rearrange("(k p) n -> p k n", p=128)[:, :, m * 128:(m + 1) * 128])
                for (off, w) in subs:
                    pb, pk = bank()
                    for j in range(KC):
                        S.op("pe", lambda e, pb=pb, j=j: e.matmul(
                            pb[:, 0:w], wv[:, j, :], act[:, j, off:off + w], start=(j == 0), stop=(j == KC - 1)),
                            r=wk + [("act", j, off)], w=[pk])
                    S.op("dve", lambda e, pb=pb: e.tensor_tensor(
                        out=xT[:, m, off:off + w], in0=pb[:, 0:w], in1=xT[:, m, off:off + w], op=ALU.add),
                        r=[pk, ("x", m, off)], w=[("x", m, off)])

        def ffn(l, subs, L):
            rmsnorm(subs, C_GFFN + l * 8)
            base = C_FCONV + l * 66
            for fc in range(FC):
                wv, wk = load_w([128, KC, 2, 128], slab_cols(ffn_w_up_d[l], 2, DFF, fc * 128))
                ci = cv_ring.next()
                ck = ("cF", l, fc)
                conv_head_init(ci, carryF[:, l, fc, :], ck)
                prev = "h"
                for (off, w) in subs:
                    (Ap, Ak), (Gp, Gk) = bank(), bank()
                    for g, (pb, pk) in enumerate(((Ap, Ak), (Gp, Gk))):
                        for k in range(KC):
                            S.op("pe", lambda e, pb=pb, g=g, k=k: e.matmul(
                                pb[:, 0:w], wv[:, k, g, :], xn[:, k, off:off + w], start=(k == 0), stop=(k == KC - 1)),
                                r=wk + [("xn", k, off)], w=[pk])
                    S.op("act", lambda e, Ap=Ap: e.activation(
                        out=cv_t[:, ci, 2 + off:2 + off + w], in_=Ap[:, 0:w], func=AF.Copy),
                        r=[Ak], w=[("cv", ci, off)])
                    ui = u_ring.next()
                    uk = ("u", ui)
                    S.op("dve", lambda e, ui=ui: e.tensor_scalar(
                        out=u_t[:, ui, 0:w], in0=cv_t[:, ci, off:off + w], scalar1=ccol(base + fc), scalar2=None,
                        op0=ALU.mult), r=[("cv", ci, off), ("cv", ci, prev), "consts"], w=[uk])
                    S.op("dve", lambda e, ui=ui: e.scalar_tensor_tensor(
                        out=u_t[:, ui, 0:w], in0=cv_t[:, ci, off + 1:off + 1 + w], scalar=ccol(base + 22 + fc),
                        in1=u_t[:, ui, 0:w], op0=ALU.mult, op1=ALU.add),
                        r=[("cv", ci, off), ("cv", ci, prev), uk], w=[uk])
                    S.op("dve", lambda e, ui=ui, Ap=Ap: e.scalar_tensor_tensor(
                        out=u_t[:, ui, 0:w], in0=Ap[:, 0:w], scalar=ccol(base + 44 + fc),
                        in1=u_t[:, ui, 0:w], op0=ALU.mult, op1=ALU.add),
                        r=[Ak, uk], w=[uk])
                    si = s_ring.next()
                    sk = ("s", si)
                    S.op("act", lambda e, ui=ui, si=si: e.activation(
                        out=s_t[:, si, 0:w], in_=u_t[:, ui, 0:w], func=AF.Silu), r=[uk], w=[sk])
                    S.op("dve", lambda e, si=si, Gp=Gp: e.tensor_tensor(
                        out=act[:, fc, off:off + w], in0=Gp[:, 0:w], in1=s_t[:, si, 0:w], op=ALU.mult),
                        r=[Gk, sk], w=[("act", fc, off)])
                    prev = off
                conv_tail_save(ci, carryF[:, l, fc, :], ck, L, prev)
            for m in range(KC):
                wv, wk = load_w([128, FC, 128],
                                ffn_w_down_d[l].rearrange("(k p) n -> p k n", p=128)[:, :, m * 128:(m + 1) * 128])
                for (off, w) in subs:
                    pb, pk = bank()
                    for fc in range(FC):
                        S.op("pe", lambda e, pb=pb, fc=fc: e.matmul(
                            pb[:, 0:w], wv[:, fc, :], act[:, fc, off:off + w], start=(fc == 0), stop=(fc == FC - 1)),
                            r=wk + [("act", fc, off)], w=[pk])
                    S.op("dve", lambda e, pb=pb: e.tensor_tensor(
                        out=xT[:, m, off:off + w], in0=pb[:, 0:w], in1=xT[:, m, off:off + w], op=ALU.add),
                        r=[pk, ("x", m, off)], w=[("x", m, off)])

        def xkeys(c, subs):
            return [("x", c, off) for (off, w) in subs]

        def load_x2(src2d, t0, L, subs):
            for c in range(KC):
                S.dma("sp", lambda e, c=c: e.dma_start(
                    out=xT[:, c, 0:L], in_=src2d[c * 128:(c + 1) * 128, t0:t0 + L]), ("xl", c),
                    w=xkeys(c, subs))

        def store_x(dst2d, t0, L, subs, src_off=0):
            for c in range(KC):
                S.dma("sp", lambda e, c=c: e.dma_start(
                    out=dst2d[c * 128:(c + 1) * 128, t0:t0 + L], in_=xT[:, c, src_off:src_off + L]), ("xs", c),
                    r=xkeys(c, subs))

        def kv_proj(tq0, subs, L, ci_chunk):
            rmsnorm(subs, C_GKV)
            ksubs = [(off, w) for (off, w) in subs if w == 512]
            for m in range(KC):
                wv, wk = load_w([128, KC, 128],
                                w_kvf_d.rearrange("(k p) n -> p k n", p=128)[:, :, m * 128:(m + 1) * 128])
                for (off, w) in ksubs:
                    pb, pk = bank()
                    for k in range(KC):
                        S.op("pe", lambda e, pb=pb, k=k: e.matmul(
                            pb[:, 0:w], wv[:, k, :], xn[:, k, off:off + w], start=(k == 0), stop=(k == KC - 1)),
                            r=wk + [("xn", k, off)], w=[pk])
                    ri, rk = head_norm_rs(pb, pk, w, EPS, 1.0 / HD)
                    ti = st_ring.next()
                    tk = ("st", ti)
                    S.op("dve", lambda e, pb=pb, ri=ri, ti=ti: e.scalar_tensor_tensor(
                        out=st_t[:, ti, 0:w], in0=pb[:, 0:w], scalar=ccol(C_GK), in1=rs_t[:, ri, 0:w],
                        op0=ALU.mult, op1=ALU.mult), r=[pk, rk, "consts"], w=[tk])
                    c0_ = tq0 + off - HALO
                    S.dma("sp", lambda e, ti=ti, m=m, c0_=c0_, w=w: e.dma_start(
                        out=Ks_d[m * 128:(m + 1) * 128, c0_:c0_ + w], in_=st_t[:, ti, 0:w]),
                        ("st", ti), r=[tk])
            cqi = ci_chunk % 2
            hoff = subs[0][1] if subs[0][1] != 512 else 0
            if hoff:
                S.op("dve", lambda e: e.memset(cq_t[:, cqi, 0:hoff], 0.0), w=[("cq", cqi, "halo")])
            tiles = [(off + i * 128, off) for (off, w) in ksubs for i in range(4)]
            for tt, (o, so) in enumerate(tiles):
                tok = tq0 + o - HALO
                jloc = tok // 128
                (V0, V0k), (V1, V1k), (Fp, Fk) = bank(), bank(), bank()
                for n, (pb, pk) in enumerate(((V0, V0k), (V1, V1k))):
                    for k in range(KC):
                        S.op("pe", lambda e, pb=pb, n=n, k=k: e.matmul(
                            pb[:, :], xn[:, k, o:o + 128], WV[:, k, n * 512:(n + 1) * 512],
                            start=(k == 0), stop=(k == KC - 1)), r=["WV", ("xn", k, so)], w=[pk])
                for k in range(KC):
                    S.op("pe", lambda e, k=k: e.matmul(
                        Fp[:, 0:NH], xn[:, k, o:o + 128], WV[:, k, D:D + NH], start=(k == 0), stop=(k == KC - 1)),
                        r=["WV", ("xn", k, so)], w=[Fk])
                vi = vst_ring.next()
                vk = ("vst", vi)
                S.op("act", lambda e, vi=vi: e.activation(out=vst_t[:, vi, 0:512], in_=V0[:, :], func=AF.Copy),
                     r=[V0k], w=[(vk, 0)])
                S.op("dve", lambda e, vi=vi: e.tensor_copy(out=vst_t[:, vi, 512:1024], in_=V1[:, :]),
                     r=[V1k], w=[(vk, 1)])
                S.dma("sp", lambda e, vi=vi, tok=tok: e.dma_start(
                    out=Vs_d[tok:tok + 128, :], in_=vst_t[:, vi, :]), ("vst", vi), r=[(vk, 0), (vk, 1)])
                zi = zf_ring.next()
                zk = ("zf", zi)
                S.op("dve", lambda e, zi=zi: e.tensor_tensor(
                    out=zf_t[:, zi, 0, :], in0=Fp[:, 0:NH], in1=consts[:, C_BF:C_BF + NH], op=ALU.add),
                    r=[Fk, "consts"], w=[(zk, 0)])
                S.op("act", lambda e, zi=zi: e.activation(
                    out=zf_t[:, zi, 1, :], in_=zf_t[:, zi, 0, :], func=AF.Exp, scale=-1.0), r=[(zk, 0)], w=[(zk, 1)])
                S.op("act", lambda e, zi=zi: e.activation(
                    out=zf_t[:, zi, 2, :], in_=zf_t[:, zi, 1, :], func=AF.Ln, bias=1.0, scale=1.0),
                    r=[(zk, 1)], w=[(zk, 2)])
                (Cs, Csk), (Ct, Ctk) = bank(), bank()
                S.op("pe", lambda e, zi=zi: e.matmul(Cs[:, 0:NH], tri_f, zf_t[:, zi, 2, :], start=True, stop=False),
                     r=[(zk, 2), "cmat"], w=[Csk])
                S.op("pe", lambda e: e.matmul(Cs[:, 0:NH], ones_f, sacc[:], start=False, stop=True),
                     r=["sacc", "cmat"], w=[Csk])
                S.op("pe", lambda e, zi=zi: e.matmul(Ct[0:NH, 0:128], zf_t[:, zi, 2, :], tri_f, start=True, stop=False),
                     r=[(zk, 2), "cmat"], w=[Ctk])
                S.op("pe", lambda e: e.matmul(Ct[0:NH, 0:128], sacc[:], ones_f, start=False, stop=True),
                     r=["sacc", "cmat"], w=[Ctk])
                S.op("dve", lambda e, jloc=jloc: e.tensor_copy(out=bias_all[:, NJo + jloc, :], in_=Cs[:, 0:NH]),
                     r=[Csk], w=[("bias", NJo + jloc)])
                S.op("act", lambda e, o=o: e.mul(out=cq_t[:, cqi, o:o + 128], in_=Ct[0:NH, 0:128], mul=-1.0),
                     r=[Ctk], w=[("cq", cqi, tt)])
                S.op("dve", lambda e, zi=zi: e.tensor_tensor(
                    out=sacc[:], in0=sacc[:], in1=zf_t[:, zi, 2, :], op=ALU.add), r=[(zk, 2), "sacc"], w=["sacc"])
            S.dma("sp", lambda e: e.dma_start(out=CQ_d[:, tq0:tq0 + L], in_=cq_t[:, cqi, 0:L]), ("cq", cqi),
                  r=[("cq", cqi, tt) for tt in range(len(tiles))] + ([("cq", cqi, "halo")] if hoff else []))

        def publish_and_gather():
            pb, pk = bank()
            S.op("pe", lambda e: e.matmul(pb[:, 0:NH], ones_f, sacc[:], start=True, stop=True),
                 r=["sacc", "cmat"], w=[pk])
            S.op("dve", lambda e: e.tensor_copy(out=totb[:], in_=pb[:, 0:NH]), r=[pk], w=["totb"])
            S.dma("sp", lambda e: e.dma_start(
                out=Dd_d[:, 0:NJo * NH], in_=bias_all[:, NJo:2 * NJo, :]), "pd0",
                r=[("bias", NJo + j) for j in range(NJo)])
            S.dma("sp", lambda e: e.dma_start(out=Dd_d[:, NJo * NH:ND], in_=totb[:]), "pd1", r=["totb"])
            S.barrier(lambda e: e.memset(scratch[:], 0.0))
            for name, src, dst, rows in (("ccK", Ks_d, gK_d, RK), ("ccV", Vs_d, gV_d, RV), ("ccD", Dd_d, gD_d, 128)):
                tot = src.shape[0]
                for k in range(tot // rows):
                    S.dma("pool", lambda e, src=src, dst=dst, k=k, rows=rows: e.collective_compute(
                        "AllGather", ALU.bypass, replica_groups=pairs, ins=[src[k * rows:(k + 1) * rows, :]],
                        outs=[dst[2 * k * rows:2 * (k + 1) * rows, :]]), name, inc=1)

        def copy_other():
            def fn(e, sem):
                par = e.partition_id() % 2
                for mine, src_half in ((0, 1), (1, 0)):
                    with e.If(par == mine):
                        for (o_d, g_d, rows) in ((oK_d, gK_d, RK), (oV_d, gV_d, RV), (oD_d, gD_d, 128)):
                            for k in range(o_d.shape[0] // rows):
                                b0 = (2 * k + src_half) * rows
                                e.dma_start(out=o_d[k * rows:(k + 1) * rows, :],
                                            in_=g_d[b0:b0 + rows, :]).then_inc(sem, 16)
            S.dma("sp", fn, "oth", inc=16 * (D // RK + SO // RV + 1), self_inc=True)

        def other_bias():
            S.dma("sp", lambda e: e.dma_start(out=dpeer[:], in_=oD_d), "dpeer", w=["dpeer"])
            for j in range(NJo):
                S.op("dve", lambda e, j=j: e.scalar_tensor_tensor(
                    out=bias_all[:, j, :], in0=dpeer[:, j * NH:(j + 1) * NH], scalar=ccol(C_PEN + j),
                    in1=dpeer[:, NJo * NH:ND], op0=ALU.add, op1=ALU.subtract),
                    r=["dpeer", "consts"], w=[("bias", j)])

        def qg_proj(j, tq0, subs):
            rmsnorm(subs, C_GATT + (2 + j) * 8)
            wq2 = b_w_qg_d[j].rearrange("(k p) n -> p k n", p=128)
            for m in range(KC):
                wv, wk = load_w([128, KC, 128], wq2[:, :, m * 128:(m + 1) * 128])
                for (off, w) in subs:
                    pb, pk = bank()
                    for k in range(KC):
                        S.op("pe", lambda e, pb=pb, k=k: e.matmul(
                            pb[:, 0:w], wv[:, k, :], xn[:, k, off:off + w], start=(k == 0), stop=(k == KC - 1)),
                            r=wk + [("xn", k, off)], w=[pk])
                    ri, rk = head_norm_rs(pb, pk, w, HD * EPS, 1.0)
                    ti = st_ring.next()
                    tk = ("st", ti)
                    S.op("dve", lambda e, pb=pb, ri=ri, ti=ti: e.scalar_tensor_tensor(
                        out=st_t[:, ti, 0:w], in0=pb[:, 0:w], scalar=ccol(C_GQ + j), in1=rs_t[:, ri, 0:w],
                        op0=ALU.mult, op1=ALU.mult), r=[pk, rk, "consts"], w=[tk])
                    S.dma("sp", lambda e, ti=ti, m=m, off=off, w=w: e.dma_start(
                        out=Qs_d[m * 128:(m + 1) * 128, tq0 + off:tq0 + off + w], in_=st_t[:, ti, 0:w]),
                        ("st", ti), r=[tk])
            for m in range(KC):
                wv, wk = load_w([128, KC, 128], wq2[:, :, D + m * 128:D + (m + 1) * 128])
                for (off, w) in subs:
                    pb, pk = bank()
                    for k in range(KC):
                        S.op("pe", lambda e, pb=pb, k=k: e.matmul(
                            pb[:, 0:w], wv[:, k, :], xn[:, k, off:off + w], start=(k == 0), stop=(k == KC - 1)),
                            r=wk + [("xn", k, off)], w=[pk])
                    ti = st_ring.next()
                    tk = ("st", ti)
                    S.op("act", lambda e, pb=pb, ti=ti: e.activation(
                        out=st_t[:, ti, 0:w], in_=pb[:, 0:w], func=AF.Sigmoid), r=[pk], w=[tk])
                    S.dma("sp", lambda e, ti=ti, m=m, off=off, w=w: e.dma_start(
                        out=Gs_d[m * 128:(m + 1) * 128, tq0 + off:tq0 + off + w], in_=st_t[:, ti, 0:w]),
                        ("st", ti), r=[tk])

        def attention():
            Kv = [AA[:, s * SV:(s + 1) * SV] for s in range(2)]
            Pt = AA[:, 2 * SV:2 * SV + 4096].rearrange("p (n t) -> p n t", n=8)
            AXb = AX[:, :].bitcast(BF16)
            Vv = [AXb[:, s * 8192: s * 8192 + NJ * 128].rearrange("p (j d) -> p j d", d=128) for s in range(2)]
            Qv = [AN[:, s * QN:(s + 1) * QN] for s in range(2)]
            Gv = [AW[:, s * QN:(s + 1) * QN] for s in range(2)]
            p_ring = Ring("p", 8)
            s_banks = [0, 1, 2, 3, 6, 7]
            o_banks = [4, 5]
            sb_ring = Ring("sb", 6)
            ob_ring = Ring("ob", 2)
            og_ring = Ring("og", 2)
            for s in range(2):
                S.op("dve", lambda e, s=s: e.memset(Kv[s][64:65, :], 1.0), w=[("K1", s)])
                S.op("dve", lambda e, s=s: e.memset(Vv[s][:, :, 64:128], 1.0), w=[("V1", s)])
            blocks = [(0, HALO)] + [(HALO + 512 * i, 512) for i in range(SO // 512)]
            def head_loads(h):
                s = h % 2
                kK, kV, kQ, kG = ("K", s), ("V", s), ("Q", s), ("G", s)
                S.dma("sp", lambda e: e.dma_start(out=Kv[s][0:64, 0:SO], in_=oK_d[h * 64:(h + 1) * 64, :]),
                      ("Ko", s), w=[(kK, 0)])
                S.dma("sp", lambda e: e.dma_start(out=Kv[s][0:64, SO:2 * SO], in_=Ks_d[h * 64:(h + 1) * 64, :]),
                      ("K", s), w=[(kK, 1)])
                S.dma("sp", lambda e: e.dma_start(
                    out=Vv[s][:, 0:NJo, 0:64],
                    in_=oV_d[:, h * 64:(h + 1) * 64].rearrange("(j p) d -> p j d", p=128)),
                    ("Vo", s), w=[(kV, 0)])
                S.dma("sp", lambda e: e.dma_start(
                    out=Vv[s][:, NJo:2 * NJo, 0:64],
                    in_=Vs_d[:, h * 64:(h + 1) * 64].rearrange("(j p) d -> p j d", p=128)),
                    ("V", s), w=[(kV, 1)])
                S.dma("sp", lambda e: e.dma_start(out=Qv[s][0:64, :], in_=Qs_d[h * 64:(h + 1) * 64, :]),
                      ("Q", s), w=[(kQ, 0)])
                S.dma("sp", lambda e: e.dma_start(out=Qv[s][64:65, :], in_=CQ_d[h:h + 1, :]),
                      ("Q2", s), w=[(kQ, 1)])
                S.dma("sp", lambda e: e.dma_start(out=Gv[s][0:64, :], in_=Gs_d[h * 64:(h + 1) * 64, :]),
                      ("G", s), w=[kG])

            items = []
            for h in range(NH):
                for (tq0, w) in blocks:
                    qa0 = Q0 + tq0
                    jlist = []
                    for j in range(NJ):
                        if j * 128 > qa0 + w - 1:
                            break
                        if j * 128 + 127 <= qa0:
                            jlist.append((j, 0, None))
                        else:
                            offs = qa0 - j * 128
                            jlist.append((j, max(0, -offs), offs + 384))
                    for n, (j, cs, c0) in enumerate(jlist):
                        items.append(dict(h=h, tq0=tq0, w=w, j=j, cs=cs, c0=c0, first=(n == 0),
                                          last=(n == len(jlist) - 1), newhead=(n == 0 and tq0 == 0)))

            def qk(it):
                h, tq0, w, j, cs = it["h"], it["tq0"], it["w"], it["j"], it["cs"]
                s = h % 2
                si = s_banks[sb_ring.next()]
                Sb, Sbk = banks[si], ("ps", si)
                it["Sb"], it["Sbk"] = Sb, Sbk
                S.op("pe", lambda e: e.matmul(
                    Sb[:, cs:w], Kv[s][0:65, j * 128:(j + 1) * 128], Qv[s][0:65, tq0 + cs:tq0 + w],
                    start=True, stop=True), r=[(("K", s), 0), (("K", s), 1), ("K1", s), (("Q", s), 0), (("Q", s), 1)], w=[Sbk])

            cur = {}

            def rest(it):
                h, tq0, w, j, cs, c0 = it["h"], it["tq0"], it["w"], it["j"], it["cs"], it["c0"]
                s = h % 2
                if it["newhead"] and h + 1 < NH:
                    head_loads(h + 1)
                Sb, Sbk = it["Sb"], it["Sbk"]
                kV, kG = ("V", s), ("G", s)
                if it["first"]:
                    oi = o_banks[ob_ring.next()]
                    cur["Ob"], cur["Obk"] = banks[oi], ("ps", oi)
                Ob, Obk = cur["Ob"], cur["Obk"]
                pi = p_ring.next()
                pk_ = ("P", pi)
                S.op("act", lambda e: e.activation(
                    out=Pt[:, pi, cs:w], in_=Sb[:, cs:w], func=AF.Exp, bias=bias_all[:, j, h:h + 1], scale=1.0),
                    r=[Sbk, ("bias", j)], w=[pk_])
                if c0 is not None:
                    S.op("dve", lambda e: e.tensor_tensor(
                        out=Pt[:, pi, cs:w], in0=Pt[:, pi, cs:w], in1=mask_bf[:, c0 + cs:c0 + w], op=ALU.mult),
                        r=[pk_, "mask_bf"], w=[pk_])
                first, last = it["first"], it["last"]
                S.op("pe", lambda e: e.matmul(
                    Ob[:, cs:w], Vv[s][:, j, :], Pt[:, pi, cs:w], start=first, stop=last),
                    r=[(kV, 0), (kV, 1), ("V1", s), pk_], w=[Obk])
                if not last:
                    return
                gi = og_ring.next()
                S.op("dve", lambda e: e.tensor_scalar(
                    out=rd_t[:, gi, 0:w], in0=Ob[64:128, 0:w], scalar1=1e-30, scalar2=None, op0=ALU.add),
                    r=[Obk], w=[("rd", gi)])
                S.op("dve", lambda e: e.reciprocal(out=rd_t[:, gi, 0:w], in_=rd_t[:, gi, 0:w]),
                     r=[("rd", gi)], w=[("rd", gi)])
                S.op("dve", lambda e: e.tensor_tensor(
                    out=on_t[:, gi, 0:w], in0=Ob[0:64, 0:w], in1=rd_t[:, gi, 0:w], op=ALU.mult),
                    r=[Obk, ("rd", gi)], w=[("on", gi)])
                ti = st_ring.next()
                tk = ("st", ti)
                if tq0 == 0:
                    S.op("dve", lambda e: e.scalar_tensor_tensor(
                        out=st_t[0:64, ti, 0:w], in0=on_t[:, gi, 0:w], scalar=consts[0:64, C_HS:C_HS + 1],
                        in1=Gv[s][0:64, tq0:tq0 + w], op0=ALU.mult, op1=ALU.mult),
                        r=[("on", gi), kG, "consts"], w=[tk])
                else:
                    S.op("dve", lambda e: e.tensor_tensor(
                        out=st_t[0:64, ti, 0:w], in0=on_t[:, gi, 0:w], in1=Gv[s][0:64, tq0:tq0 + w], op=ALU.mult),
                        r=[("on", gi), kG], w=[tk])
                S.dma("sp", lambda e: e.dma_start(
                    out=Os_d[h * 64:(h + 1) * 64, tq0:tq0 + w], in_=st_t[0:64, ti, 0:w]), ("st", ti), r=[tk])

            head_loads(0)
            LA = 5
            for n in range(min(LA, len(items))):
                qk(items[n])
            for n in range(len(items)):
                if n + LA < len(items):
                    qk(items[n + LA])
                rest(items[n])

        def load_og(tq0, L, subs):
            for c in range(KC):
                S.dma("sp", lambda e, c=c: e.dma_start(
                    out=act[:, c, 0:L], in_=Os_d[c * 128:(c + 1) * 128, tq0:tq0 + L]), ("ogl", c),
                    w=[("act", c, off) for (off, w) in subs])

        own = []
        tq = 0
        for i in range(SO // TC):
            L = TC + (HALO if i == 0 else 0)
            subs = ([(0, HALO)] if i == 0 else []) + [((HALO if i == 0 else 0) + o, 512) for o in range(0, TC, 512)]
            own.append((tq, L, subs))
            tq += L

        for ci_, (tq0, L, subs) in enumerate(own):
            load_x2(xT_d, tq0, L, subs)
            for l in range(2):
                mixer(l, subs, L)
                ffn(l, subs, L)
            store_x(Xs_d, tq0, L, subs)
            kv_proj(tq0, subs, L, ci_)
        publish_and_gather()

        for (tq0, L, subs) in own:
            load_x2(Xs_d, tq0, L, subs)
            qg_proj(0, tq0, subs)
        S.barrier(lambda e: e.memset(scratch[:], 0.0))
        copy_other()
        S.barrier(lambda e: e.memset(scratch[:], 0.0))
        other_bias()
        attention()
        S.barrier(lambda e: e.memset(scratch[:], 0.0))
        for (tq0, L, subs) in own:
            load_x2(Xs_d, tq0, L, subs)
            load_og(tq0, L, subs)
            out_proj(b_w_out_d[0], subs)
            ffn(2, subs, L)
            store_x(Xs_d, tq0, L, subs)
            qg_proj(1, tq0, subs)
        S.barrier(lambda e: e.memset(scratch[:], 0.0))
        attention()
        S.barrier(lambda e: e.memset(scratch[:], 0.0))
        for i, (tq0, L, subs) in enumerate(own):
            load_x2(Xs_d, tq0, L, subs)
            load_og(tq0, L, subs)
            out_proj(b_w_out_d[1], subs)
            ffn(3, subs, L)
            if i == 0:
                store_x(out_d, 0, L - HALO, subs, src_off=HALO)
            else:
                store_x(out_d, tq0 - HALO, L, subs)

        S.emit(nc, es)
    return nc


def host_consts(SV, SO, half, attn_norm, ffn_norm, a_conv, kv_norm, b_f, k_norm, q_norm, ffn_conv):
    NJo = SO // 128
    c = np.zeros((128, C_PEN + NJo), np.float32)
    c[:, C_GATT:C_GATT + 32] = attn_norm.reshape(4, 8, 128).transpose(2, 0, 1).reshape(128, 32)
    c[:, C_GFFN:C_GFFN + 32] = ffn_norm.reshape(4, 8, 128).transpose(2, 0, 1).reshape(128, 32)
    c[:, C_GKV:C_GKV + 8] = kv_norm.reshape(8, 128).T
    c[:, C_ACONV:C_ACONV + 48] = a_conv.reshape(2, 3, 8, 128).transpose(3, 0, 1, 2).reshape(128, 48)
    c[:, C_FCONV:C_FCONV + 264] = ffn_conv.reshape(4, 3, 22, 128).transpose(3, 0, 1, 2).reshape(128, 264)
    c[:, C_GK] = np.tile(k_norm, 2)
    c[:, C_GQ] = np.tile(q_norm[0], 2)
    c[:, C_GQ + 1] = np.tile(q_norm[1], 2)
    c[:, C_HS] = 1.0 if half == 1 else 0.0
    c[:, C_BF:C_BF + NH] = np.broadcast_to(b_f[None, :], (128, NH))
    if half == 0:
        c[:, C_PEN:] = np.float32(PEN)
    return c


def host_cmat():
    m = np.zeros((128, 384 + 896), np.float32)
    m[:, 0:128] = 1.0
    m[0:64, 128:192] = 1.0
    m[64:128, 192:256] = 1.0
    m[:, 256:384] = np.triu(np.ones((128, 128), np.float32))
    k = np.arange(128)[:, None]
    cc = np.arange(896)[None, :]
    m[:, 384:] = (k <= cc - 384).astype(np.float32)
    return m


def make_in_maps(SV, SO, x_halves, attn_norm, ffn_norm, a_w_in, a_conv, a_w_out, kv_norm, w_kvf, b_f, k_norm,
                 b_w_qg, q_norm, b_w_out, ffn_w_up, ffn_conv, ffn_w_down):
    cm = host_cmat()
    f = lambda a: np.ascontiguousarray(np.asarray(a, dtype=np.float32))
    shared = {"cmat": cm, "a_w_in": f(a_w_in), "a_w_out": f(a_w_out), "w_kvf": f(w_kvf), "b_w_qg": f(b_w_qg),
              "b_w_out": f(b_w_out), "ffn_w_up": f(ffn_w_up), "ffn_w_down": f(ffn_w_down)}
    maps = []
    for half, xT in x_halves:
        m = dict(shared)
        m["xT"] = np.ascontiguousarray(xT, dtype=np.float32)
        m["consts"] = host_consts(SV, SO, half, f(attn_norm), f(ffn_norm), f(a_conv), f(kv_norm), f(b_f), f(k_norm),
                                  f(q_norm), f(ffn_conv))
        maps.append(m)
    return maps


_NC_CACHE = {}


def kernel(x, attn_norm, ffn_norm, a_w_in, a_conv, a_w_out, kv_norm, w_kvf, b_f, k_norm, b_w_qg, q_norm, b_w_out,
           ffn_w_up, ffn_conv, ffn_w_down):
    x = np.asarray(x, dtype=np.float32)
    B, SEQ, _ = x.shape
    SO = SEQ // 2
    SV = SEQ
    TC = 1024
    halves = []
    for b in range(B):
        for half in range(2):
            if half == 1:
                xT = x[b, SO - HALO:].T
            else:
                xT = np.concatenate([np.zeros((D, HALO), np.float32), x[b, :SO].T], axis=1)
            halves.append((half, xT))
    in_maps = make_in_maps(SV, SO, halves, attn_norm, ffn_norm, a_w_in, a_conv, a_w_out, kv_norm, w_kvf, b_f, k_norm,
                           b_w_qg, q_norm, b_w_out, ffn_w_up, ffn_conv, ffn_w_down)
    key = (SV, SO, TC)
    if key not in _NC_CACHE:
        _NC_CACHE[key] = build_nc(SV, SO, TC, n_cores=len(in_maps))
    nc = _NC_CACHE[key]
    res = run_bass_kernel_spmd(nc, in_maps, core_ids=list(range(len(in_maps))))
    out = np.empty((B, SEQ, D), np.float32)
    for i, r in enumerate(res.results):
        b, half = divmod(i, 2)
        out[b, half * SO:(half + 1) * SO, :] = r["outT"].T
    return out
```
